# Optimizing a Trainium2 kernel written in Bass

```python
import math
import jax, jax.numpy as jnp
from jax import lax
import numpy as np

D_MODEL = 1024
BATCH = 16
SEQ = 2048
DEPTH = 4

CTX_LEN = 256
GRID_W = 64
N_EVEN = (DEPTH + 1) // 2
N_ODD = DEPTH // 2

SSD_HEADS = 16
SSD_HEAD_DIM = 64
SSD_INNER = SSD_HEADS * SSD_HEAD_DIM
SSD_GROUPS = 2
SSD_HPG = SSD_HEADS // SSD_GROUPS
SSD_STATE = 128
SSD_CHUNK = 128
SSD_CONV = 5
SSD_CONV_CH = SSD_INNER + 2 * SSD_GROUPS * SSD_STATE
CONV_CH = 1024
CONV_WIDTH = 31
MLP_CH = 1024
MLP_GROUPS = 8
MLP_GROUP_CH = MLP_CH // MLP_GROUPS
MLP_CHUNK = 128
ATT_HEADS = 16
ATT_KV_HEADS = 4
ATT_REP = ATT_HEADS // ATT_KV_HEADS
ATT_HEAD_DIM = 64
ATT_WINDOW = 128
ATT_BLOCK = 128
ATT_SCALE = ATT_HEAD_DIM ** -0.5
ROPE_BASE = 10000.0

EVEN_SIZES = (SSD_INNER, SSD_CONV_CH, 2 * SSD_HEADS, 2 * CONV_CH, CONV_CH)
EVEN_IN = sum(EVEN_SIZES)
EVEN_MIX = SSD_INNER + CONV_CH
ODD_SIZES = (MLP_CH, MLP_CH, MLP_CH, ATT_HEADS * ATT_HEAD_DIM, ATT_KV_HEADS * ATT_HEAD_DIM,
             ATT_KV_HEADS * ATT_HEAD_DIM, ATT_HEADS * ATT_HEAD_DIM)
ODD_IN = sum(ODD_SIZES)
ODD_MIX = MLP_CH + ATT_HEADS * ATT_HEAD_DIM

DEEPNORM_ALPHA = (2 * DEPTH) ** 0.25
DEEPNORM_BETA = (8 * DEPTH) ** -0.25
LN_EPS = 1e-5

kernel_name = 'hybrid_ssd_conformer_gmlp_swa_dit_trunk'

F32 = jnp.float32


def _split(t, sizes):
    offs = np.cumsum(sizes)[:-1].tolist()
    return jnp.split(t, offs, axis=-1)


def layer_norm(t, g, b):
    tf = t.astype(F32)
    mu = jnp.mean(tf, -1, keepdims=True)
    var = jnp.mean(jnp.square(tf - mu), -1, keepdims=True)
    return ((tf - mu) * lax.rsqrt(var + LN_EPS)).astype(t.dtype) * g + b


def rms_norm(t, w):
    tf = t.astype(F32)
    return (tf * lax.rsqrt(jnp.mean(tf * tf, -1, keepdims=True) + LN_EPS)).astype(t.dtype) * w


def dwconv(t, w, b):
    pad = w.shape[0] // 2
    y = lax.conv_general_dilated(t, w[:, None, :].astype(t.dtype), window_strides=(1,), padding=[(pad, pad)],
                                 dimension_numbers=('NWC', 'WIO', 'NWC'), feature_group_count=t.shape[-1])
    return y + b


def modulation(cond, w, b):
    m = jax.nn.silu(cond) @ w + b
    sh, sc, g = jnp.split(m, 3, axis=-1)
    return sh[:, None, :], sc[:, None, :], g[:, None, :]


def axial_rope(rows):
    t = jnp.arange(rows * GRID_W)
    row = (t // GRID_W).astype(F32)
    col = (t % GRID_W).astype(F32)
    n_freq = ATT_HEAD_DIM // 4
    inv = ROPE_BASE ** (-jnp.arange(n_freq, dtype=F32) / n_freq)
    ang = jnp.concatenate([row[:, None] * inv, col[:, None] * inv], -1)
    return jnp.cos(ang), jnp.sin(ang)


def apply_rope(t, cos, sin):
    tf = t.astype(F32)
    h = ATT_HEAD_DIM // 2
    t1, t2 = tf[..., :h], tf[..., h:]
    c = cos[None, :, None, :]
    s = sin[None, :, None, :]
    return jnp.concatenate([t1 * c - t2 * s, t2 * c + t1 * s], -1).astype(t.dtype)


def ssd_chunked(x, dt, a, bm, cm, h0):
    bsz, L, G, R, P = x.shape
    N = bm.shape[-1]
    Q = SSD_CHUNK
    T = L // Q
    xdt = (x * dt[..., None]).reshape(bsz, T, Q, G, R, P)
    bc = bm.reshape(bsz, T, Q, G, N)
    cc = cm.reshape(bsz, T, Q, G, N)
    a_cum = jnp.cumsum((dt * a).reshape(bsz, T, Q, G, R), axis=2)
    seg = a_cum[:, :, :, None] - a_cum[:, :, None, :]
    lower = jnp.tril(jnp.ones((Q, Q), bool))[:, :, None, None]
    decay = jnp.exp(jnp.where(lower, seg, -jnp.inf))
    cb = jnp.einsum('btlgn,btsgn->btlsg', cc, bc).astype(F32)
    y_diag = jnp.einsum('btlsgr,btsgrp->btlgrp', cb[..., None] * decay, xdt)
    decay_to_end = jnp.exp(a_cum[:, :, -1:] - a_cum)
    states = jnp.einsum('btsgn,btsgr,btsgrp->btgrpn', bc, decay_to_end, xdt).astype(F32)
    chunk_decay = jnp.exp(a_cum[:, :, -1])

    def step(h, inp):
        s, d = inp
        return h * d[..., None, None] + s, h

    h_final, h_in = lax.scan(step, h0, (jnp.moveaxis(states, 1, 0), jnp.moveaxis(chunk_decay, 1, 0)))
    h_in = jnp.moveaxis(h_in, 0, 1)
    y_off = jnp.einsum('btlgn,btgrpn,btlgr->btlgrp', cc, h_in, jnp.exp(a_cum))
    return (y_diag + y_off).reshape(bsz, L, G, R, P).astype(x.dtype), h_final


def ssd_branch(z, xbc, dt_raw, h0_f, h0_b, conv_w, conv_b, dt_bias, a, d_skip, norm_w):
    bsz, L, _ = xbc.shape
    xbc = jax.nn.silu(dwconv(xbc, conv_w, conv_b))
    xs, bm, cm = _split(xbc, (SSD_INNER, SSD_GROUPS * SSD_STATE, SSD_GROUPS * SSD_STATE))
    xs = xs.reshape(bsz, L, SSD_GROUPS, SSD_HPG, SSD_HEAD_DIM)
    bm = bm.reshape(bsz, L, SSD_GROUPS, SSD_STATE)
    cm = cm.reshape(bsz, L, SSD_GROUPS, SSD_STATE)
    dt = jax.nn.softplus((dt_raw + dt_bias).astype(F32))
    dt_f = dt[..., :SSD_HEADS].reshape(bsz, L, SSD_GROUPS, SSD_HPG)
    dt_b = dt[..., SSD_HEADS:].reshape(bsz, L, SSD_GROUPS, SSD_HPG)
    a_f = a[0].reshape(SSD_GROUPS, SSD_HPG)
    a_b = a[1].reshape(SSD_GROUPS, SSD_HPG)
    y_f, h_f = ssd_chunked(xs, dt_f, a_f, bm, cm, h0_f)
    fl = lambda t: jnp.flip(t, axis=1)
    y_b, h_b = ssd_chunked(fl(xs), fl(dt_b), a_b, fl(bm), fl(cm), h0_b)
    y = y_f + fl(y_b) + xs * d_skip.reshape(SSD_GROUPS, SSD_HPG, 1)
    y = y.reshape(bsz, L, SSD_INNER)
    return rms_norm(y * jax.nn.silu(z), norm_w), h_f, h_b


def conformer_conv(glu_in, w, b, ln_g, ln_b):
    val, gt = jnp.split(glu_in, 2, axis=-1)
    h = dwconv(val * jax.nn.sigmoid(gt), w, b)
    return jax.nn.silu(layer_norm(h, ln_g, ln_b))


def even_mixer(h_x, h_c, w_in, conv_w, conv_b, dt_bias, a_log, d_skip, ssd_norm,
               cv_w, cv_b, cv_ln_g, cv_ln_b, w_out, with_ctx):
    a = -jnp.exp(a_log.astype(F32))
    z_c, xbc_c, dt_c, glu_c, gate_c = _split(h_c @ w_in, EVEN_SIZES)
    z_x, xbc_x, dt_x, glu_x, gate_x = _split(h_x @ w_in, EVEN_SIZES)
    zero = jnp.zeros((h_c.shape[0], SSD_GROUPS, SSD_HPG, SSD_HEAD_DIM, SSD_STATE), F32)
    ya_c, hf_c, hb_c = ssd_branch(z_c, xbc_c, dt_c, zero, zero, conv_w, conv_b, dt_bias, a, d_skip, ssd_norm)
    ya_x, _, _ = ssd_branch(z_x, xbc_x, dt_x, hf_c, hb_c, conv_w, conv_b, dt_bias, a, d_skip, ssd_norm)

    def merge(ya, glu, gate):
        yb = conformer_conv(glu, cv_w, cv_b, cv_ln_g, cv_ln_b) * jax.nn.silu(gate)
        return jnp.concatenate([ya, yb], axis=-1) @ w_out

    y_x = merge(ya_x, glu_x, gate_x)
    y_c = merge(ya_c, glu_c, gate_c) if with_ctx else None
    return y_x, y_c


def chunk_mlp(u, v, ln_g, ln_b, ws, bs):
    u = jax.nn.gelu(u)
    v = layer_norm(jax.nn.gelu(v), ln_g, ln_b)
    bsz, L, _ = v.shape
    v = v.reshape(bsz, L // MLP_CHUNK, MLP_CHUNK, MLP_GROUPS, MLP_GROUP_CH)
    mixed = jnp.einsum('gqp,bcpgd->bcqgd', ws, v) + bs.T[None, None, :, :, None]
    return u * mixed.reshape(bsz, L, MLP_CH)


def window_attention(q, k, v, k_ctx, v_ctx, sink):
    bsz, L = q.shape[:2]
    W = ATT_BLOCK
    nb = L // W

    def band_blocks(t):
        tp = jnp.pad(t, ((0, 0), (W, W), (0, 0), (0, 0))).reshape(bsz, nb + 2, W, ATT_KV_HEADS, ATT_HEAD_DIM)
        return jnp.moveaxis(jnp.concatenate([tp[:, :-2], tp[:, 1:-1], tp[:, 2:]], axis=2), 1, 0)

    kb = band_blocks(k)
    vb = band_blocks(v)
    qb = jnp.moveaxis(q.reshape(bsz, nb, W, ATT_KV_HEADS, ATT_REP, ATT_HEAD_DIM), 1, 0)
    rel = jnp.arange(3 * W) - W
    band = jnp.abs(jnp.arange(W)[:, None] - rel[None, :]) <= ATT_WINDOW
    key_pos = jnp.arange(nb)[:, None] * W + rel[None, :]
    mask = band[None] & ((key_pos >= 0) & (key_pos < L))[:, None, :]
    sink_f = sink.astype(F32)[None, :, :, None, None]

    def block(args):
        qblk, kblk, vblk, mblk = args
        s_lat = jnp.einsum('bqkrd,bskd->bkrqs', qblk, kblk).astype(F32) * ATT_SCALE
        s_lat = jnp.where(mblk, s_lat, -jnp.inf)
        s_ctx = jnp.einsum('bqkrd,bskd->bkrqs', qblk, k_ctx).astype(F32) * ATT_SCALE
        s_snk = jnp.broadcast_to(sink_f, s_ctx.shape[:-1] + (1,))
        p = jax.nn.softmax(jnp.concatenate([s_lat, s_ctx, s_snk], axis=-1), axis=-1).astype(vblk.dtype)
        return (jnp.einsum('bkrqs,bskd->bqkrd', p[..., :3 * W], vblk)
                + jnp.einsum('bkrqs,bskd->bqkrd', p[..., 3 * W:-1], v_ctx))

    out = lax.map(block, (qb, kb, vb, mask))
    return jnp.moveaxis(out, 0, 1).reshape(bsz, L, ATT_HEADS * ATT_HEAD_DIM)


def context_attention(qc, kc, vc, sink):
    bsz, C = qc.shape[:2]
    s = jnp.einsum('bqkrd,bskd->bkrqs', qc, kc).astype(F32) * ATT_SCALE
    s_snk = jnp.broadcast_to(sink.astype(F32)[None, :, :, None, None], s.shape[:-1] + (1,))
    p = jax.nn.softmax(jnp.concatenate([s, s_snk], axis=-1), axis=-1)[..., :-1].astype(vc.dtype)
    return jnp.einsum('bkrqs,bskd->bqkrd', p, vc).reshape(bsz, C, ATT_HEADS * ATT_HEAD_DIM)


def odd_mixer(h_x, h_c, w_in, mlp_ln_g, mlp_ln_b, ws, bs, sink, w_out, cos, sin, with_ctx):
    u_x, v_x, gc_x, q_x, k_x, va_x, gd_x = _split(h_x @ w_in, ODD_SIZES)
    u_c, v_c, gc_c, q_c, k_c, va_c, gd_c = _split(h_c @ w_in, ODD_SIZES)
    bsz, L = h_x.shape[:2]
    C = h_c.shape[1]
    heads = lambda t, n: t.reshape(t.shape[0], t.shape[1], n, ATT_HEAD_DIM)
    sink_g = sink.reshape(ATT_KV_HEADS, ATT_REP)
    k_ctx = heads(k_c, ATT_KV_HEADS)
    v_ctx = heads(va_c, ATT_KV_HEADS)
    q = apply_rope(heads(q_x, ATT_HEADS), cos, sin).reshape(bsz, L, ATT_KV_HEADS, ATT_REP, ATT_HEAD_DIM)
    k = apply_rope(heads(k_x, ATT_KV_HEADS), cos, sin)
    yd_x = window_attention(q, k, heads(va_x, ATT_KV_HEADS), k_ctx, v_ctx, sink_g)
    yc_x = chunk_mlp(u_x, v_x, mlp_ln_g, mlp_ln_b, ws, bs)
    y_x = jnp.concatenate([yc_x * jax.nn.silu(gc_x), yd_x * jax.nn.silu(gd_x)], axis=-1) @ w_out
    if with_ctx:
        qc = heads(q_c, ATT_HEADS).reshape(bsz, C, ATT_KV_HEADS, ATT_REP, ATT_HEAD_DIM)
        yd_c = context_attention(qc, k_ctx, v_ctx, sink_g)
        yc_c = chunk_mlp(u_c, v_c, mlp_ln_g, mlp_ln_b, ws, bs)
        y_c = jnp.concatenate([yc_c * jax.nn.silu(gc_c), yd_c * jax.nn.silu(gd_c)], axis=-1) @ w_out
    else:
        y_c = None
    return y_x, y_c


def setup_inputs(seed: int = 0) -> dict:
    key = jax.random.key(seed)
    ks = iter(jax.random.split(key, 40))
    nrm = lambda shape, scale: jax.random.normal(next(ks), shape, F32) * scale
    E, O, D = N_EVEN, N_ODD, D_MODEL
    dt0 = jnp.exp(jax.random.uniform(next(ks), (E, 2 * SSD_HEADS), F32, math.log(1e-3), math.log(1e-1)))
    dt_bias = dt0 + jnp.log(-jnp.expm1(-dt0))
    a_log = jnp.log(jax.random.uniform(next(ks), (E, 2, SSD_HEADS), F32, 1.0, 16.0))
    return {
        'x': nrm((BATCH, SEQ, D), 1.0),
        'c': nrm((BATCH, D), 1.0),
        'ctx': nrm((BATCH, CTX_LEN, D), 1.0),
        'c_ctx': nrm((D,), 1.0),
        'mod_w': nrm((DEPTH, D, 3 * D), D ** -0.5),
        'mod_b': nrm((DEPTH, 3 * D), 0.02),
        'ln_g': 1.0 + nrm((DEPTH, D), 0.02),
        'ln_b': nrm((DEPTH, D), 0.02),
        'ev_w_in': nrm((E, D, EVEN_IN), D ** -0.5),
        'ev_ssd_conv_w': nrm((E, SSD_CONV, SSD_CONV_CH), SSD_CONV ** -0.5),
        'ev_ssd_conv_b': nrm((E, SSD_CONV_CH), 0.02),
        'ev_dt_bias': dt_bias,
        'ev_a_log': a_log,
        'ev_d_skip': 1.0 + nrm((E, SSD_HEADS), 0.02),
        'ev_ssd_norm': 1.0 + nrm((E, SSD_INNER), 0.02),
        'ev_cv_w': nrm((E, CONV_WIDTH, CONV_CH), CONV_WIDTH ** -0.5),
        'ev_cv_b': nrm((E, CONV_CH), 0.02),
        'ev_cv_ln_g': 1.0 + nrm((E, CONV_CH), 0.02),
        'ev_cv_ln_b': nrm((E, CONV_CH), 0.02),
        'ev_w_out': nrm((E, EVEN_MIX, D), EVEN_MIX ** -0.5 * DEEPNORM_BETA),
        'od_w_in': nrm((O, D, ODD_IN), D ** -0.5),
        'od_mlp_ln_g': 1.0 + nrm((O, MLP_CH), 0.02),
        'od_mlp_ln_b': nrm((O, MLP_CH), 0.02),
        'od_ws': nrm((O, MLP_GROUPS, MLP_CHUNK, MLP_CHUNK), MLP_CHUNK ** -0.5),
        'od_bs': 1.0 + nrm((O, MLP_GROUPS, MLP_CHUNK), 0.02),
        'od_sink': nrm((O, ATT_HEADS), 0.5),
        'od_w_out': nrm((O, ODD_MIX, D), ODD_MIX ** -0.5 * DEEPNORM_BETA),
    }


def reference(x, c, ctx, c_ctx, mod_w, mod_b, ln_g, ln_b,
              ev_w_in, ev_ssd_conv_w, ev_ssd_conv_b, ev_dt_bias, ev_a_log, ev_d_skip, ev_ssd_norm,
              ev_cv_w, ev_cv_b, ev_cv_ln_g, ev_cv_ln_b, ev_w_out,
              od_w_in, od_mlp_ln_g, od_mlp_ln_b, od_ws, od_bs, od_sink, od_w_out):
    ROWS = x.shape[1] // GRID_W
    cos, sin = axial_rope(ROWS)
    for layer in range(DEPTH):
        last = layer == DEPTH - 1
        i = layer // 2
        sh_x, sc_x, g_x = modulation(c, mod_w[layer], mod_b[layer])
        sh_c, sc_c, g_c = modulation(c_ctx[None, :], mod_w[layer], mod_b[layer])
        h_x = x * (1.0 + sc_x) + sh_x
        h_c = ctx * (1.0 + sc_c) + sh_c
        if layer % 2 == 0:
            y_x, y_c = even_mixer(h_x, h_c, ev_w_in[i], ev_ssd_conv_w[i], ev_ssd_conv_b[i], ev_dt_bias[i],
                                  ev_a_log[i], ev_d_skip[i], ev_ssd_norm[i], ev_cv_w[i], ev_cv_b[i],
                                  ev_cv_ln_g[i], ev_cv_ln_b[i], ev_w_out[i], not last)
        else:
            y_x, y_c = odd_mixer(h_x, h_c, od_w_in[i], od_mlp_ln_g[i], od_mlp_ln_b[i], od_ws[i], od_bs[i],
                                 od_sink[i], od_w_out[i], cos, sin, not last)
        x = layer_norm(DEEPNORM_ALPHA * x + g_x * y_x, ln_g[layer], ln_b[layer])
        if not last:
            ctx = layer_norm(DEEPNORM_ALPHA * ctx + g_c * y_c, ln_g[layer], ln_b[layer])
    return x
```

```python
import math
from contextlib import ExitStack
import numpy as np
import concourse.bass as bass
import concourse.mybir as mybir
from concourse.bass_utils import run_bass_kernel_spmd

F32 = mybir.dt.float32
BF16 = mybir.dt.bfloat16
AF = mybir.ActivationFunctionType
ALU = mybir.AluOpType

D = 1024
DEPTH = 4
NB = 2
CTX = 256
SEQ = 2048
NT = CTX + SEQ
NCH = NT // 128
EVEN_IN = 5664
ODD_IN = 5632
ALPHA = (2 * DEPTH) ** 0.25
EPS = 1e-5
ATT_SCALE = 64 ** -0.5
NEG = -30000.0
FM_TILES = [(0, 256)] + [(256 + 512 * i, 512) for i in range(4)]


class Buf:
    __slots__ = ("name", "w", "r", "excl")

    def __init__(self, name="", excl=False):
        self.name = name
        self.w = None
        self.r = []
        self.excl = excl


class V:
    __slots__ = ("ap", "b")

    def __init__(self, ap, b):
        self.ap = ap
        self.b = b

    def __getitem__(self, k):
        return V(self.ap[k], self.b)

    def bc(self, shape):
        return V(self.ap.broadcast_to(shape), self.b)

    def re(self, s, **kw):
        return V(self.ap.rearrange(s, **kw), self.b)


class Op:
    __slots__ = ("eng", "fn", "deps", "signal", "val", "sem", "dma", "idx", "snap", "waits")

    def __init__(self, eng, fn, dma):
        self.idx = 0
        self.snap = None
        self.waits = None
        self.eng = eng
        self.fn = fn
        self.deps = []
        self.signal = False
        self.val = 0
        self.sem = None
        self.dma = dma


ENGS = ("pe", "act", "dve", "pool", "sp")


class Sched:
    def __init__(self, nc, n_dma_sems=28):
        self.nc = nc
        self.ops = {e: [] for e in ENGS}
        self.n_dma_sems = n_dma_sems
        self.dma_rr = {"sp": 0, "pool": 0, "act": 0}
        self.dma_last = {}
        self.out_dmas = []

    def add(self, eng, fn, reads=(), writes=(), dma=False):
        op = Op(eng, fn, dma)
        self.nrec = getattr(self, "nrec", 0) + 1
        op.idx = self.nrec
        deps = set()
        R = [b for v in reads for b in v.b]
        W = [b for v in writes for b in v.b]
        for b in R:
            if b.w is not None:
                deps.add(b.w)
            if b.excl:
                for r_ in b.r:
                    if r_.eng != eng:
                        deps.add(r_)
        for b in W:
            if b.w is not None:
                deps.add(b.w)
            deps.update(b.r)
        for d in deps:
            if d.eng == "pe" and eng == "pe" and not d.dma and not dma:
                continue
            op.deps.append(d)
            d.signal = True
        if dma:
            k = (eng, self.dma_rr[eng] % self.n_dma_sems)
            self.dma_rr[eng] += 1
            prev = self.dma_last.get(k)
            if prev is not None:
                op.deps.append(prev)
            self.dma_last[k] = op
            op.sem = k
            op.signal = True
        for b in W:
            b.w = op
            b.r = []
        Ws = set(id(b) for b in W)
        for b in R:
            if id(b) not in Ws:
                if not dma:
                    b.r = [x for x in b.r if x.dma or x.eng != eng]
                b.r.append(op)
        self.ops[eng].append(op)
        return op

    def barrier(self):
        lasts = []
        for e in ENGS:
            for op in reversed(self.ops[e]):
                if not op.dma and op.fn is not None:
                    lasts.append(op)
                    break
        lasts += list(self.dma_last.values())
        for e in ENGS:
            op = Op(e, None, False)
            for d in lasts:
                if d.eng == e and not d.dma:
                    continue
                op.deps.append(d)
                d.signal = True
            self.ops[e].append(op)

    def emit(self):
        nc = self.nc
        with ExitStack() as es:
            sems = {}
            for e in ENGS:
                sems[e] = es.enter_context(nc.semaphore("s_" + e))
            for e in ("sp", "pool"):
                for i in range(self.n_dma_sems):
                    sems[(e, i)] = es.enter_context(nc.semaphore("d_%s_%d" % (e, i)))
            cnt = {k: 0 for k in sems}
            for e in ENGS:
                for op in self.ops[e]:
                    if op.dma:
                        cnt[op.sem] += 16
                        op.val = cnt[op.sem]
                    elif op.signal:
                        op.sem = e
                        cnt[e] += 1
                        op.val = cnt[e]
            self.maxvals = cnt
            allops = sorted((op for e in ENGS for op in self.ops[e]), key=lambda o: o.idx)
            kn = {e: {} for e in ENGS}
            for op in allops:
                k = kn[op.eng]
                need = {}
                src = {}
                for d in op.deps:
                    if need.get(d.sem, 0) < d.val:
                        need[d.sem] = d.val
                        src[d.sem] = d
                op.waits = []
                for s_, v_ in sorted(need.items(), key=lambda kv: -src[kv[0]].idx):
                    if k.get(s_, 0) >= v_:
                        continue
                    op.waits.append((s_, v_))
                    k[s_] = v_
                    sn = src[s_].snap
                    if sn:
                        for s2, v2 in sn.items():
                            if k.get(s2, 0) < v2:
                                k[s2] = v2
                if op.signal or op.dma:
                    k2 = dict(k)
                    op.snap = k2
            block = es.enter_context(nc.Block())
            finals = self.out_dmas

            def run(e, engobj):
                known = {}
                for op in self.ops[e]:
                    todo = list(op.waits)
                    attach = None
                    if todo and op.fn is not None and not op.dma and e in ("pe", "act", "dve"):
                        attach = todo.pop()
                    for s, v in todo:
                        engobj.wait_ge(sems[s], v)
                        known[s] = v
                    if op.fn is None:
                        continue
                    ins = op.fn(engobj)
                    if attach is not None:
                        ins._wait_ge(sems[attach[0]], attach[1])
                        known[attach[0]] = attach[1]
                    if op.dma:
                        ins.then_inc(sems[op.sem], 16)
                    elif op.signal:
                        ins.then_inc(sems[op.sem], 1)
                if e == "sp":
                    for op in finals:
                        if known.get(op.sem, 0) < op.val:
                            engobj.wait_ge(sems[op.sem], op.val)
                            known[op.sem] = op.val

            @block.sync
            def _(e):
                run("sp", e)

            @block.tensor
            def _(e):
                run("pe", e)

            @block.scalar
            def _(e):
                run("act", e)

            @block.vector
            def _(e):
                run("dve", e)

            @block.gpsimd
            def _(e):
                run("pool", e)


class Prog:
    def __init__(self, n_layers=DEPTH, debug=False, stop_after=0, use_barrier=False):
        self.stop_after = stop_after
        self.use_barrier = use_barrier
        self.n_layers = n_layers
        self.debug = debug
        self.nc = bass.Bass("TRN2", target_bir_lowering=False)
        self.S = Sched(self.nc)
        self.es = ExitStack()
        self.dram_bufs = {}
        self.dram_vs = {}
        self.uid = 0

    def dram_in(self, name, shape):
        return self.nc.dram_tensor(name, list(shape), F32, kind="ExternalInput").ap()

    def setup_mem(self):
        nc = self.nc
        ARENA_F32 = 46080
        self.arena = self.es.enter_context(nc.sbuf_tensor("arena", [128, ARENA_F32], F32))
        self.arena_sz = ARENA_F32
        self.arena_off = 0
        self.top_blocks = []
        self.live_tiles = []
        self.pref = {}
        self.pers = self.es.enter_context(nc.sbuf_tensor("pers", [128, 3 * 1024], F32))
        self.pers_off = 0
        self.psum = self.es.enter_context(nc.psum_tensor("psum", [128, 4096], F32))
        self.pbank = [Buf("bank%d" % i, excl=True) for i in range(8)]

    def _carve(self, base, off, shape, dt):
        n = int(np.prod(shape[1:]))
        words = (n + 1) // 2 if dt == BF16 else n
        words = (words + 7) // 8 * 8
        if dt == BF16:
            ap = base[0:shape[0], off:off + words].bitcast(BF16)[:, 0:n]
        else:
            ap = base[0:shape[0], off:off + n]
        if len(shape) == 3:
            ap = ap.rearrange("p (a b) -> p a b", a=shape[1])
        elif len(shape) == 4:
            ap = ap.rearrange("p (a b c) -> p a b c", a=shape[1], b=shape[2])
        return ap, words

    def _inherit(self, off, words, name):
        nb = Buf(name)
        keep = []
        for (o, w_, ob) in self.live_tiles:
            if o < off + words and off < o + w_:
                if ob.w is not None:
                    nb.r.append(ob.w)
                nb.r.extend(ob.r)
                if not (off <= o and o + w_ <= off + words):
                    keep.append((o, w_, ob))
            else:
                keep.append((o, w_, ob))
        keep.append((off, words, nb))
        self.live_tiles = keep
        return nb

    def A(self, shape, dt=F32, name="t"):
        off = self.arena_off
        ap, words = self._carve(self.arena, off, shape, dt)
        self.arena_off += words
        lim = min([o for (o, w_) in self.top_blocks], default=self.arena_sz)
        assert self.arena_off <= lim, ("arena overflow", name, self.arena_off, lim)
        return V(ap, [self._inherit(off, words, name)])

    def A_top(self, shape, dt=F32, name="t"):
        n = int(np.prod(shape[1:]))
        words = ((n + 1) // 2 if dt == BF16 else n)
        words = (words + 7) // 8 * 8
        low = min([o for (o, w_) in self.top_blocks], default=self.arena_sz)
        off = low - words
        assert off >= self.arena_off, ("arena top overflow", name, off, self.arena_off)
        ap, _ = self._carve(self.arena, off, shape, dt)
        blk = (off, words)
        self.top_blocks.append(blk)
        return V(ap, [self._inherit(off, words, name)]), blk

    def release_top(self, blk):
        self.top_blocks.remove(blk)

    def Pm(self, shape, dt=F32, name="p"):
        ap, words = self._carve(self.pers, self.pers_off, shape, dt)
        self.pers_off += words
        assert self.pers_off <= 3 * 1024, ("pers overflow", name, self.pers_off)
        return V(ap, [Buf(name)])

    def stage_end(self):
        if self.use_barrier:
            self.S.barrier()
        self.arena_off = 0
        self._rows = None
        self.nstage = getattr(self, "nstage", 0) + 1
        if self.stop_after and self.nstage >= self.stop_after:
            raise StopIteration

    def PS(self, bank, off=0, n=512):
        a = bank * 512 + off
        b0 = a // 512
        b1 = (a + n - 1) // 512
        return V(self.psum[:, a:a + n], [self.pbank[i] for i in range(b0, b1 + 1)])

    def PS16(self, bank):
        return V(self.psum[:, bank * 512:(bank + 1) * 512].bitcast(BF16), [self.pbank[bank]])

    def scratch(self, name, shape, dt):
        h = self.nc.dram_tensor(name, list(shape), dt, kind="Internal").ap()
        self.dram_bufs[name] = {}
        return (name, h)

    def _db(self, name, b, t0, t1):
        d = self.dram_bufs[name]
        out = []
        for blk in range(t0 // 128, (t1 + 127) // 128):
            k = (b, blk)
            if k not in d:
                d[k] = Buf("%s_%d_%d" % (name, b, blk))
            out.append(d[k])
        return out

    def fm(self, sc, b, t0, t1, r0=0, r1=None):
        name, h = sc
        if r1 is None:
            r1 = h.shape[1]
        ap = h[b, r0:r1, t0:t1].rearrange("(c p) t -> p c t", p=128)
        return self._dv(V(ap, self._db(name, b, t0, t1)))

    def tm(self, sc, b, t0, t1, c0=0, c1=None):
        name, h = sc
        if c1 is None:
            c1 = h.shape[2]
        return self._dv(V(h[b, t0:t1, c0:c1], self._db(name, b, t0, t1)))

    def blk(self, sc, b, t):
        name, h = sc
        return self._dv(V(h[b, t], self._db(name, b, t * 128, t * 128 + 128)))

    def _dv(self, v):
        self.dram_vs[id(v.b)] = v.b
        return v

    def dma(self, out, in_, eng=None):
        o, i = out.ap, in_.ap
        if eng is None:
            eng = "pool" if id(out.b) in self.dram_vs else "sp"
        return self.S.add(eng, lambda e: e.dma_start(out=o, in_=i), reads=[in_], writes=[out], dma=True)

    def mm(self, out, lhsT, rhs, start=True, stop=True):
        o, l, r = out.ap, lhsT.ap, rhs.ap
        return self.S.add("pe", lambda e: e.matmul(o, lhsT=l, rhs=r, start=start, stop=stop),
                          reads=[lhsT, rhs], writes=[out])

    def tr(self, out, in_, ident):
        o, i, d = out.ap, in_.ap, ident.ap
        return self.S.add("pe", lambda e: e.transpose(o, i, d), reads=[in_, ident], writes=[out])

    def act(self, out, in_, func, bias=None, scale=None, accum=None):
        o, i = out.ap, in_.ap
        kw = {}
        reads = [in_]
        if bias is not None:
            if isinstance(bias, V):
                kw["bias"] = bias.ap
                reads.append(bias)
            else:
                kw["bias"] = float(bias)
        if scale is not None:
            if isinstance(scale, V):
                kw["scale"] = scale.ap
                reads.append(scale)
            else:
                kw["scale"] = float(scale)
        writes = [out]
        if accum is not None:
            kw["accum_out"] = accum.ap
            writes.append(accum)
        return self.S.add("act", lambda e: e.activation(out=o, in_=i, func=func, **kw), reads=reads, writes=writes)

    def tt(self, out, in0, in1, op, eng="dve"):
        o, a, b = out.ap, in0.ap, in1.ap
        return self.S.add(eng, lambda e: e.tensor_tensor(out=o, in0=a, in1=b, op=op), reads=[in0, in1], writes=[out])

    def ts(self, out, in0, s1, op0, s2=None, op1=None, eng="dve"):
        o, a = out.ap, in0.ap
        reads = [in0]
        x1 = s1
        if isinstance(s1, V):
            reads.append(s1)
            x1 = s1.ap
        x2 = s2
        if isinstance(s2, V):
            reads.append(s2)
            x2 = s2.ap
        if op1 is None:
            return self.S.add(eng, lambda e: e.tensor_scalar(out=o, in0=a, scalar1=x1, scalar2=None, op0=op0),
                              reads=reads, writes=[out])
        return self.S.add(eng, lambda e: e.tensor_scalar(out=o, in0=a, scalar1=x1, scalar2=x2, op0=op0, op1=op1),
                          reads=reads, writes=[out])

    def stt(self, out, in0, scalar, in1, op0, op1):
        o, a, b = out.ap, in0.ap, in1.ap
        reads = [in0, in1]
        x = scalar
        if isinstance(scalar, V):
            reads.append(scalar)
            x = scalar.ap
        return self.S.add("dve", lambda e: e.scalar_tensor_tensor(out=o, in0=a, scalar=x, in1=b, op0=op0, op1=op1),
                          reads=reads, writes=[out])

    def cp(self, out, in_, eng="dve"):
        o, i = out.ap, in_.ap
        if eng == "act":
            return self.S.add("act", lambda e: e.copy(out=o, in_=i), reads=[in_], writes=[out])
        return self.S.add(eng, lambda e: e.tensor_copy(out=o, in_=i), reads=[in_], writes=[out])

    def memset(self, out, val, eng="dve"):
        o = out.ap
        return self.S.add(eng, lambda e: e.memset(o, val), writes=[out])

    def recip(self, out, in_):
        o, i = out.ap, in_.ap
        return self.S.add("dve", lambda e: e.reciprocal(out=o, in_=i), reads=[in_], writes=[out])

    def evac(self, out, in_, k):
        return self.cp(out, in_, eng=("act" if k % 2 else "dve"))

    def build(self):
        nc = self.nc
        P = self
        self.setup_mem()
        I = {}
        I["x"] = self.dram_in("x", [NB, SEQ, D])
        I["c"] = self.dram_in("c", [NB, D])
        I["ctx"] = self.dram_in("ctx", [NB, CTX, D])
        I["c_ctx"] = self.dram_in("c_ctx", [1, D])
        I["mod_w"] = self.dram_in("mod_w", [DEPTH, D, 3 * D])
        I["mod_b"] = self.dram_in("mod_b", [DEPTH, 3 * D])
        I["ln_g"] = self.dram_in("ln_g", [DEPTH, D])
        I["ln_b"] = self.dram_in("ln_b", [DEPTH, D])
        I["ev_w_in"] = self.dram_in("ev_w_in", [2, D, EVEN_IN])
        I["ev_ssd_conv_w"] = self.dram_in("ev_ssd_conv_w", [2, 5, 1536])
        I["ev_ssd_conv_b"] = self.dram_in("ev_ssd_conv_b", [2, 1536])
        I["ev_dt_bias"] = self.dram_in("ev_dt_bias", [2, 32])
        I["ev_a_log"] = self.dram_in("ev_a_log", [2, 32])
        I["ev_d_skip"] = self.dram_in("ev_d_skip", [2, 16])
        I["ev_ssd_norm"] = self.dram_in("ev_ssd_norm", [2, 1024])
        I["ev_cv_w"] = self.dram_in("ev_cv_w", [2, 31, 1024])
        I["ev_cv_b"] = self.dram_in("ev_cv_b", [2, 1024])
        I["ev_cv_ln_g"] = self.dram_in("ev_cv_ln_g", [2, 1024])
        I["ev_cv_ln_b"] = self.dram_in("ev_cv_ln_b", [2, 1024])
        I["ev_w_out"] = self.dram_in("ev_w_out", [2, 2048, D])
        I["od_w_in"] = self.dram_in("od_w_in", [2, D, ODD_IN])
        I["od_mlp_ln_g"] = self.dram_in("od_mlp_ln_g", [2, 1024])
        I["od_mlp_ln_b"] = self.dram_in("od_mlp_ln_b", [2, 1024])
        I["od_ws"] = self.dram_in("od_ws", [2, 8, 128, 128])
        I["od_bs"] = self.dram_in("od_bs", [2, 1024])
        I["od_sink"] = self.dram_in("od_sink", [2, 16])
        I["od_w_out"] = self.dram_in("od_w_out", [2, 2048, D])
        I["k_ident"] = self.dram_in("k_ident", [128, 128])
        I["k_tri"] = self.dram_in("k_tri", [128, 128])
        I["k_trirev"] = self.dram_in("k_trirev", [128, 128])
        I["k_negf"] = self.dram_in("k_negf", [128, 512])
        I["k_negb"] = self.dram_in("k_negb", [128, 512])
        I["k_perm"] = self.dram_in("k_perm", [128, 128])
        I["k_hmask"] = self.dram_in("k_hmask", [128, 2])
        I["k_cos"] = self.dram_in("k_cos", [128, SEQ])
        I["k_sin"] = self.dram_in("k_sin", [128, SEQ])
        self.I = I
        if self.debug:
            self.out = nc.dram_tensor("out", [NB, D, NT], F32, kind="ExternalOutput").ap()
        else:
            self.out = nc.dram_tensor("out", [NB, SEQ, D], F32, kind="ExternalOutput").ap()
        self.Idb = Buf("inputs")

        def IN(ap):
            return V(ap, [])

        self.IN = IN
        self.XT = self.scratch("XT", [NB, D, NT], F32)
        self.XBC = self.scratch("XBC", [NB, 1536, NT], BF16)
        self.U = self.scratch("U", [NB, 1024, NT], BF16)
        self.G = self.scratch("G", [NB, 1024, NT], F32)
        self.Z = self.scratch("Z", [NB, NT, 1024], F32)
        self.DT = self.scratch("DT", [NB, NT, 32], F32)
        self.MIX = self.scratch("MIX", [NB, 2048, NT], BF16)
        self.YD = self.scratch("YD", [NB, NCH, 128, 1024], F32)
        self.SF = self.scratch("SF", [NB, NCH, 128, 1024], F32)
        self.SB = self.scratch("SB", [NB, NCH, 128, 1024], F32)
        self.HF = self.scratch("HF", [NB, NCH, 128, 1024], BF16)
        self.HB = self.scratch("HB", [NB, NCH, 128, 1024], BF16)
        self.CTs = self.scratch("CTs", [NB, NCH, 128, 256], BF16)
        self.EE = self.scratch("EE", [NB, NCH, 128, 32], F32)
        self.CD = self.scratch("CD", [NB, NCH, 128, 32], F32)
        self.QT = self.scratch("QT", [NB, 1024, NT], BF16)
        self.KT = self.scratch("KT", [NB, 512, NT], BF16)
        self.VA = self.scratch("VA", [NB, NT, 256], BF16)
        self.GD = self.scratch("GD", [NB, NT, 1024], F32)
        self.AT = self.scratch("AT", [NB, 1024, NT], F32)

        self.ident = P.Pm([128, 128], F32, "ident")
        self.ident16 = P.Pm([128, 128], BF16, "ident16")
        self.tri = P.Pm([128, 128], F32, "tri")
        self.trirev = P.Pm([128, 128], F32, "trirev")
        self.negf = P.Pm([128, 512], F32, "negf")
        self.negb = P.Pm([128, 512], F32, "negb")
        self.negf16 = P.Pm([128, 512], BF16, "negf16")
        self.negb16 = P.Pm([128, 512], BF16, "negb16")
        self.perm16 = P.Pm([128, 128], BF16, "perm16")
        self.ones16 = P.Pm([128, 128], BF16, "ones16")
        self.tri16 = P.Pm([128, 128], BF16, "tri16")
        self.trirev16 = P.Pm([128, 128], BF16, "trirev16")
        self.negfb16 = P.Pm([128, 512], BF16, "negfb16")
        self.ones32 = P.Pm([128, 128], F32, "ones32")
        self.one1 = P.Pm([1, 8], F32, "one1")
        self.modT = P.Pm([128, 24, 3], F32, "modT")
        self.sc1 = P.Pm([128, 8, 3], F32, "sc1")
        self.lng = P.Pm([128, 8], F32, "lng")
        self.lnb = P.Pm([128, 8], F32, "lnb")
        self.colA = P.Pm([128, 64], F32, "colA")
        P.dma(self.ident, IN(I["k_ident"]))
        P.dma(self.ident16, IN(I["k_ident"]), eng="pool")
        P.dma(self.tri, IN(I["k_tri"]))
        P.dma(self.trirev, IN(I["k_trirev"]))
        P.dma(self.negf, IN(I["k_negf"]))
        P.dma(self.negb, IN(I["k_negb"]))
        P.dma(self.negf16, IN(I["k_negf"]), eng="pool")
        P.dma(self.negb16, IN(I["k_negb"]), eng="pool")
        P.dma(self.perm16, IN(I["k_perm"]), eng="pool")
        P.dma(self.tri16, IN(I["k_tri"]), eng="pool")
        P.dma(self.trirev16, IN(I["k_trirev"]), eng="pool")
        P.dma(self.negfb16[:, 0:128], IN(I["k_negf"][:, 0:128]), eng="pool")
        P.dma(self.negfb16[:, 128:256], IN(I["k_negb"][:, 0:128]), eng="pool")
        P.dma(self.negfb16[:, 256:384], IN(I["k_negf"][:, 0:128]), eng="pool")
        P.dma(self.negfb16[:, 384:512], IN(I["k_negb"][:, 0:128]), eng="pool")
        P.memset(self.ones32, 1.0)
        P.memset(self.ones16, 1.0)
        P.memset(self.one1, 1.0)
        self.hmask = P.Pm([128, 2], F32, "hmask")
        P.dma(self.hmask, IN(I["k_hmask"]))

        try:
            self.stage_init()
            for layer in range(self.n_layers):
                last = layer == DEPTH - 1
                self.stage_mod(layer)
                if layer % 2 == 0:
                    self.even_layer(layer // 2, layer)
                else:
                    self.odd_layer(layer // 2, layer, last)
                self.stage_out(layer, last)
        except StopIteration:
            pass
        self.stage_final()
        self.S.emit()
        self.es.close()
        return nc

    def cols(self, dst, src_row_ap, n):
        P = self
        if not getattr(self, "_rows", None):
            self._rows = [P.A([1, 1536], F32, "row%d" % i) for i in range(2)]
            self._rown = 0
        row = self._rows[self._rown % 2][:, 0:n]
        self._rown += 1
        P.dma(row, P.IN(src_row_ap))
        nchunk = n // 128
        ps = P.PS(7, 0, nchunk)
        for j in range(nchunk):
            P.mm(ps[:, j:j + 1], row[0:1, j * 128:(j + 1) * 128], self.one1[0:1, 0:1])
        P.cp(dst, ps)

    def rows_to_cols(self, dst, srcs, n):
        P = self
        R = sum(a.shape[0] for a in srcs)
        nch = n // 128
        rows = P.A([R, n], F32, "rows")
        r0 = 0
        for a in srcs:
            P.dma(rows[r0:r0 + a.shape[0], :], P.IN(a))
            r0 += a.shape[0]
        per = 512 // R
        c = 0
        while c < nch:
            m = min(per, nch - c)
            ps = P.PS(7, 0, m * R)
            for j in range(m):
                P.tr(ps[:, j * R:(j + 1) * R], rows[0:R, (c + j) * 128:(c + j + 1) * 128], self.ident[0:R, 0:R])
            P.cp(dst[:, c:c + m, :].re("p a b -> p (a b)"), ps)
            c += m

    def bcast_rows(self, dst, src_row_ap):
        n = src_row_ap.shape[-1]
        self.dma(dst, self.IN(src_row_ap.broadcast_to([128, n])))

    def load_w16(self, dst, src, ncols):
        K = dst.ap.shape[1]
        for k in range(K):
            c = 0
            while c < ncols:
                w = min(2048, ncols - c)
                self.dma(dst[:, k, c:c + w], self.IN(src[k * 128:(k + 1) * 128, c:c + w]), eng="pool")
                c += w

    def stage_init(self):
        P = self
        I = self.I
        tin = [P.A([128, 1024], F32, "tin%d" % i) for i in range(2)]
        tout = [P.A([128, 8, 128], F32, "tout%d" % i) for i in range(2)]
        n = 0
        for b in range(NB):
            for t in range(NCH):
                ti, to = tin[n % 2], tout[n % 2]
                if t < 2:
                    src = I["ctx"][b, t * 128:(t + 1) * 128, :]
                else:
                    src = I["x"][b, (t - 2) * 128:(t - 1) * 128, :]
                P.dma(ti, P.IN(src))
                pb = (n % 2) * 2
                for c in range(8):
                    P.tr(P.PS(pb, c * 128, 128), ti[:, c * 128:(c + 1) * 128], self.ident)
                P.evac(to.re("p a b -> p (a b)"), P.PS(pb, 0, 1024), n)
                P.dma(P.fm(self.XT, b, t * 128, (t + 1) * 128), to)
                n += 1
        self.stage_end()

    def stage_mod(self, layer):
        P = self
        I = self.I
        if layer % 2 == 0:
            w_, blk_ = P.A_top([128, 8, EVEN_IN], BF16, "w_in")
            P.load_w16(w_, I["ev_w_in"][layer // 2], EVEN_IN)
            self.pref["w_in"] = (w_, blk_)
        else:
            self.pref["wa"] = self.load_wa(layer // 2)
        mwb = [P.A([128, 8, 512], F32, "modw%d" % i) for i in range(2)]

        def load_mw(cb_):
            for k in range(8):
                P.dma(mwb[cb_ % 2][:, k, :], P.IN(I["mod_w"][layer, k * 128:(k + 1) * 128, cb_ * 512:(cb_ + 1) * 512]))
        load_mw(0)
        mb = P.A([1, 3072], F32, "modb")
        P.dma(mb, P.IN(I["mod_b"][layer:layer + 1, :]))
        rows = []
        for i in range(3):
            r = P.A([1, 1024], F32, "cond%d" % i)
            src = I["c"][i:i + 1, :] if i < 2 else I["c_ctx"]
            P.dma(r, P.IN(src))
            P.act(r, r, AF.Silu)
            rows.append(r)
        ps = P.PS(0, 0, 24)
        for j in range(8):
            for i in range(3):
                P.mm(ps[:, j * 3 + i:j * 3 + i + 1], rows[i][0:1, j * 128:(j + 1) * 128], self.one1[0:1, 0:1])
        condT = P.A([128, 8, 3], F32, "condT")
        P.cp(condT.re("p a b -> p (a b)"), ps)
        ps2 = P.PS(1, 0, 72)
        for oc in range(24):
            o = ps2[:, oc * 3:(oc + 1) * 3]
            if oc % 4 == 0 and oc // 4 + 1 < 6:
                load_mw(oc // 4 + 1)
            mw_ = mwb[(oc // 4) % 2]
            for k in range(8):
                P.mm(o, mw_[:, k, (oc % 4) * 128:(oc % 4 + 1) * 128], condT[:, k, :], start=(k == 0), stop=False)
            P.mm(o, mb[0:1, oc * 128:(oc + 1) * 128], self.one1[0:1, 0:3], start=False, stop=True)
        P.cp(self.modT.re("p a b -> p (a b)"), ps2)
        P.ts(self.sc1.re("p a b -> p (a b)"), self.modT[:, 8:16, :].re("p a b -> p (a b)"), 1.0, ALU.add)
        P.cols(self.lng, I["ln_g"][layer:layer + 1, :], 1024)
        P.cols(self.lnb, I["ln_b"][layer:layer + 1, :], 1024)
        self.stage_end()

    def cond_idx(self, b, t0):
        return 2 if t0 < CTX else b

    def load_hT(self, b, t0, tw, xt, hT):
        P = self
        i = self.cond_idx(b, t0)
        P.dma(xt[:, :, 0:tw], P.fm(self.XT, b, t0, t0 + tw))
        for c in range(8):
            P.act(hT[:, c, 0:tw], xt[:, c, 0:tw], AF.Identity, bias=self.modT[:, c, i:i + 1], scale=self.sc1[:, c, i:i + 1])

    def even_layer(self, li, layer):
        P = self
        I = self.I
        w, wblk = self.pref.pop("w_in")
        dtb = P.A([128, 32], F32, "dtb")
        P.bcast_rows(dtb, I["ev_dt_bias"][li:li + 1, :])
        xts = [P.A([128, 8, 512], F32, "xt%d" % i) for i in range(2)]
        hTs = [P.A([128, 8, 512], BF16, "hT%d" % i) for i in range(2)]
        sto = [P.A([128, 512], F32, "sto%d" % i) for i in range(4)]
        sto16 = [P.A([128, 512], BF16, "sto16_%d" % i) for i in range(4)]
        sig = [P.A([128, 512], F32, "sig%d" % i) for i in range(2)]
        st_z = [P.A([128, 1024], F32, "sz%d" % i) for i in range(2)]
        st_dt = [P.A([128, 128], F32, "sdt%d" % i) for i in range(2)]
        OFF_Z, OFF_XBC, OFF_DT, OFF_VAL, OFF_GT, OFF_GATE = 0, 1024, 2560, 2592, 3616, 4640
        n = 0
        nb = 0
        ns = 0

        def proj_fm(off, hT, tw, bank):
            o = P.PS(bank, 0, tw)
            for k in range(8):
                P.mm(o, w[:, k, off:off + 128], hT[:, k, 0:tw], start=(k == 0), stop=(k == 7))
            return o

        nz = 0
        for b in range(NB):
            for (t0, tw) in FM_TILES:
                xt, hT = xts[n % 2], hTs[n % 2]
                P.load_hT(b, t0, tw, xt, hT)
                for fc in range(12):
                    o = proj_fm(OFF_XBC + fc * 128, hT, tw, nb % 4)
                    s_ = sto16[ns % 4]
                    ns += 1
                    P.evac(s_[:, 0:tw], o, nb)
                    nb += 1
                    P.dma(P.fm(self.XBC, b, t0, t0 + tw, fc * 128, fc * 128 + 128)[:, 0, :], s_[:, 0:tw])
                for j in range(8):
                    ov = proj_fm(OFF_VAL + j * 128, hT, tw, nb % 4)
                    nb += 1
                    og = proj_fm(OFF_GT + j * 128, hT, tw, nb % 4)
                    nb += 1
                    sg_ = sig[j % 2]
                    s_ = sto16[ns % 4]
                    ns += 1
                    P.act(sg_[:, 0:tw], og, AF.Sigmoid)
                    P.tt(s_[:, 0:tw], ov, sg_[:, 0:tw], ALU.mult)
                    P.dma(P.fm(self.U, b, t0, t0 + tw, j * 128, j * 128 + 128)[:, 0, :], s_[:, 0:tw])
                for j in range(8):
                    o = proj_fm(OFF_GATE + j * 128, hT, tw, nb % 4)
                    nb += 1
                    s_ = sto[ns % 4]
                    ns += 1
                    P.act(s_[:, 0:tw], o, AF.Silu)
                    P.dma(P.fm(self.G, b, t0, t0 + tw, j * 128, j * 128 + 128)[:, 0, :], s_[:, 0:tw])
                nsub = tw // 128
                for sub in range(nsub):
                    tok0 = t0 + sub * 128
                    sz = st_z[nz % 2]
                    nz += 1
                    for half in range(2):
                        o = P.PS(4 + half, 0, 512)
                        for k in range(8):
                            P.mm(o, hT[:, k, sub * 128:(sub + 1) * 128], w[:, k, OFF_Z + half * 512:OFF_Z + (half + 1) * 512],
                                 start=(k == 0), stop=(k == 7))
                        P.act(sz[:, half * 512:(half + 1) * 512], o, AF.Silu)
                    P.dma(P.tm(self.Z, b, tok0, tok0 + 128), sz)
                sd = st_dt[n % 2]
                for sub in range(nsub):
                    o = P.PS(6, sub * 32, 32)
                    for k in range(8):
                        P.mm(o, hT[:, k, sub * 128:(sub + 1) * 128], w[:, k, OFF_DT:OFF_DT + 32], start=(k == 0), stop=(k == 7))
                sdv = sd[:, 0:nsub * 32]
                P.tt(sdv.re("p (s f) -> p s f", f=32), P.PS(6, 0, nsub * 32).re("p (s f) -> p s f", f=32),
                     dtb.re("p (o f) -> p o f", o=1).bc([128, nsub, 32]), ALU.add)
                P.act(sdv, sdv, AF.Exp)
                P.act(sdv, sdv, AF.Ln, bias=1.0)
                for sub in range(nsub):
                    tok0 = t0 + sub * 128
                    P.dma(P.tm(self.DT, b, tok0, tok0 + 128), sd[:, sub * 32:(sub + 1) * 32])
                n += 1
        P.release_top(wblk)
        self.stage_end()
        self.ssd_pass_a(li)
        self.conformer(li)
        self.ssd_pass_b2(li)

    def seg_bounds(self, tok):
        return (0, CTX) if tok < CTX else (CTX, NT)

    def load_window(self, dst, sc, b, t0, tw, halo):
        P = self
        lo, hi = self.seg_bounds(t0)
        a = max(t0 - halo, lo)
        e = min(t0 + tw + halo, hi)
        if a > t0 - halo:
            P.memset(dst[:, :, 0:halo], 0.0)
        if e < t0 + tw + halo:
            P.memset(dst[:, :, halo + tw:halo + tw + halo], 0.0)
        P.dma(dst[:, :, a - (t0 - halo):e - (t0 - halo)], P.fm(sc, b, a, e))

    def ssd_pass_a(self, li):
        P = self
        I = self.I
        import os
        cut = int(os.environ.get("K_CUT", "99"))
        c6 = P.A([128, 12, 6], F32, "w5cols")
        P.rows_to_cols(c6, [I["ev_ssd_conv_w"][li], I["ev_ssd_conv_b"][li:li + 1, :]], 1536)
        w5 = c6
        cbcol = c6[:, :, 5]
        cbrow = P.A([128, 1280], F32, "cbrow")
        P.bcast_rows(cbrow, I["ev_ssd_conv_b"][li:li + 1, 0:1280])
        arow = P.A([128, 32], F32, "arow")
        P.bcast_rows(arow, I["ev_a_log"][li:li + 1, :])
        P.act(arow, arow, AF.Exp)
        P.ts(arow, arow, -1.0, ALU.mult)
        dskip = P.A([128, 16], F32, "dskip")
        P.bcast_rows(dskip, I["ev_d_skip"][li:li + 1, :])
        if cut <= -2:
            self.stage_end()
            return
        diag = P.A([128, 60, 128], BF16, "diag5")
        for c in range(12):
            for k in range(5):
                P.ts(diag[:, c * 5 + k, :], self.ident16, w5[:, c, k:k + 1], ALU.mult)
        if cut <= -1:
            self.stage_end()
            return
        wins = [P.A([128, 12, 132], BF16, "win%d" % i) for i in range(2)]
        dts = [P.A([128, 32], F32, "dt%d" % i) for i in range(2)]
        sets = []
        for i in range(2):
            sets.append(dict(
                dtA=P.A([128, 32], F32, "dtA%d" % i), acum=P.A([128, 32], F32, "acum%d" % i), nacum=P.A([128, 32], F32, "nacum%d" % i),
                ee=P.A([128, 32], F32, "ee%d" % i), cd=P.A([128, 32], F32, "cd%d" % i), dte=P.A([128, 32], F32, "dte%d" % i),
                sdt=P.A([128, 32], F32, "sdt%d" % i), xsf=P.A([128, 1280], F32, "xsf%d" % i), xs=P.A([128, 1280], BF16, "xs%d" % i),
                BT=P.A([128, 2, 128], BF16, "BT%d" % i), CT=P.A([128, 2, 128], BF16, "CT%d" % i), CBT=P.A([128, 2, 128], F32, "CBT%d" % i),
                dsp=[P.A([128, 32], BF16, "dsp%d_%d" % (i, j)) for j in range(3)], dr1=P.A([128, 32], F32, "dr1_%d" % i),
                xw=[P.A([128, 1024], BF16, "xw%d_%d" % (i, j)) for j in range(2)], ydt=P.A([128, 1024], F32, "ydt%d" % i),
                ydo=P.A([128, 1024], F32, "ydo%d" % i), sfo=[P.A([128, 1024], F32, "sfo%d_%d" % (i, j)) for j in range(2)]))
        Ls = [P.A([128, 128], F32, "L%d" % i) for i in range(8)]
        Ms = [P.A([128, 128], BF16, "M%d" % i) for i in range(8)]
        cbrow16 = P.A([1, 1280], BF16, "cbrow16")
        P.dma(cbrow16, P.IN(I["ev_ssd_conv_b"][li:li + 1, 0:1280]), eng="pool")
        diagD = P.A([128, 16, 128], BF16, "diagD")
        for hd in range(16):
            P.ts(diagD[:, hd, :], self.ident16, dskip[:, hd:hd + 1], ALU.mult)
        n = 0
        nev = 0
        for b in range(NB):
            for t in range(NCH):
                tok0 = t * 128
                win, dt = wins[n % 2], dts[n % 2]
                S_ = sets[n % 2]
                dtA, acum, nacum, ee, cd, dte, sdt = S_["dtA"], S_["acum"], S_["nacum"], S_["ee"], S_["cd"], S_["dte"], S_["sdt"]
                xs, BT, CT, CBT, dsp, dr1 = S_["xs"], S_["BT"], S_["CT"], S_["CBT"], S_["dsp"], S_["dr1"]
                xw, ydo, sfo = S_["xw"], S_["ydo"], S_["sfo"]
                n += 1
                P.load_window(win, self.XBC, b, tok0, 128, 2)
                P.dma(dt, P.tm(self.DT, b, tok0, tok0 + 128))
                P.tt(dtA, dt, arow, ALU.mult)
                pa = P.PS(2, 256, 64)
                P.mm(pa[:, 0:16], self.tri, dtA[:, 0:16])
                P.mm(pa[:, 16:32], self.trirev, dtA[:, 16:32])
                P.mm(pa[:, 32:64], self.ones32, dtA)
                P.cp(acum, pa[:, 0:32])
                P.ts(nacum, pa[:, 0:32], -1.0, ALU.mult)
                P.act(ee, pa[:, 0:32], AF.Exp)
                P.act(cd, pa[:, 32:64], AF.Exp)
                P.tt(dte, pa[:, 32:64], acum, ALU.subtract)
                P.act(dte, dte, AF.Exp)
                P.tt(sdt, dte, dt, ALU.mult)
                P.dma(P.blk(self.EE, b, t), ee)
                P.dma(P.blk(self.CD, b, t), cd)
                P.cp(dsp[0], dtA)
                P.tt(dr1, dtA, dsp[0], ALU.subtract)
                P.cp(dsp[1], dr1)
                for p_, (c0, c1) in enumerate(((0, 4), (4, 8), (8, 10))):
                    for c in range(c0, c1):
                        o = P.PS(p_ % 2, (c - c0) * 128, 128)
                        P.mm(o, self.ones16[0:1, 0:128], cbrow16[0:1, c * 128:(c + 1) * 128], start=True, stop=False)
                        for k in range(5):
                            P.mm(o, win[:, c, k:k + 128], diag[:, c * 5 + k, :], start=False, stop=(k == 4))
                    P.act(xs[:, c0 * 128:c1 * 128], P.PS(p_ % 2, 0, (c1 - c0) * 128), AF.Silu)
                for c in range(8, 12):
                    o = P.PS(1, (c - 8) * 128, 128)
                    for k in range(5):
                        P.mm(o, diag[:, c * 5 + k, :], win[:, c, k:k + 128], start=(k == 0), stop=(k == 4))
                    dst = BT[:, c - 8, :] if c < 10 else CT[:, c - 10, :]
                    P.act(dst, o, AF.Silu, bias=cbcol[:, c:c + 1])
                P.dma(P.blk(self.CTs, b, t), CT.re("p a b -> p (a b)"))
                for d_ in range(2):
                    P.tt(xw[d_].re("p (h d) -> p h d", h=16), xs[:, 0:1024].re("p (h d) -> p h d", h=16),
                         sdt[:, d_ * 16:(d_ + 1) * 16].re("p (h o) -> p h o", o=1).bc([128, 16, 64]), ALU.mult,
                         eng=("dve" if d_ == 0 else "pool"))
                for g in range(2):
                    P.mm(P.PS(2, g * 128, 128), BT[:, g, :], CT[:, g, :])
                P.cp(CBT.re("p a b -> p (a b)"), P.PS(2, 0, 256))

                def seg_grp(gi):
                    sb_ = P.PS(5 + gi % 3, 0, 512)
                    P.mm(sb_, self.ident16, self.negfb16, start=True, stop=False)
                    for i_ in range(4):
                        m_ = gi * 4 + i_
                        hd_, d_ = m_ // 2, m_ % 2
                        col = d_ * 16 + hd_
                        trm = self.tri16 if d_ == 0 else self.trirev16
                        for si, sp3 in enumerate(dsp[0:2]):
                            P.mm(sb_[:, i_ * 128:(i_ + 1) * 128], sp3[:, col:col + 1].bc([128, 128]), trm,
                                 start=False, stop=(i_ == 3 and si == 1))

                def ew_grp(gi):
                    sb_ = P.PS(5 + gi % 3, 0, 512)
                    for i_ in range(4):
                        m_ = gi * 4 + i_
                        hd_, d_ = m_ // 2, m_ % 2
                        col = d_ * 16 + hd_
                        L, M = Ls[m_ % 8], Ms[m_ % 8]
                        P.act(L, sb_[:, i_ * 128:(i_ + 1) * 128], AF.Exp, bias=nacum[:, col:col + 1])
                        P.stt(M, L, dt[:, col:col + 1], CBT[:, hd_ // 8, :], ALU.mult, ALU.mult)

                def y_grp(gi):
                    for i_ in range(4):
                        m_ = gi * 4 + i_
                        hd_, d_ = m_ // 2, m_ % 2
                        yo = P.PS(3 + hd_ // 8, (hd_ % 8) * 64, 64)
                        xh = xs[:, hd_ * 64:(hd_ + 1) * 64]
                        P.mm(yo, Ms[m_ % 8], xh, start=(d_ == 0), stop=False)
                        if d_ == 1:
                            P.mm(yo, diagD[:, hd_, :], xh, start=False, stop=True)

                seg_grp(0)
                for gi in range(8):
                    if gi + 1 < 8:
                        seg_grp(gi + 1)
                    ew_grp(gi)
                    y_grp(gi)
                P.cp(ydo, P.PS(3, 0, 1024), eng="act")
                P.dma(P.blk(self.YD, b, t), ydo)
                for d_ in range(2):
                    for g in range(2):
                        bk = (2 * d_ + g) % 2
                        P.mm(P.PS(bk, 0, 512), xs[:, 1024 + g * 128:1024 + (g + 1) * 128], xw[d_][:, g * 512:(g + 1) * 512])
                        P.evac(sfo[d_][:, g * 512:(g + 1) * 512], P.PS(bk, 0, 512), nev)
                        nev += 1
                    P.dma(P.blk(self.SF if d_ == 0 else self.SB, b, t), sfo[d_])
        self.stage_end()

    def ssd_pass_b1_steps(self):
        P = self
        h = [P.A([128, 1024], F32, "h%d" % i) for i in range(2)]
        h16 = [P.A([128, 1024], BF16, "h16_%d" % i) for i in range(2)]
        sld = [P.A([128, 1024], F32, "sld%d" % i) for i in range(4)]
        cdl = [P.A([128, 32], F32, "cdl%d" % i) for i in range(4)]
        seq = []
        for b in range(NB):
            orders = [list(range(NCH)), [1, 0] + list(range(NCH - 1, 1, -1))]
            for step in range(NCH):
                for d_ in range(2):
                    seq.append((b, d_, orders[d_][step], step == 0))
        LA = 3

        def issue_loads(i):
            b, d_, t, first = seq[i]
            P.dma(sld[i % 4], P.blk(self.SF if d_ == 0 else self.SB, b, t))
            P.dma(cdl[i % 4], P.blk(self.CD, b, t))

        def mk(i):
            def f():
                b, d_, t, first = seq[i]
                if i == 0:
                    for j in range(min(LA, len(seq))):
                        issue_loads(j)
                if i + LA < len(seq):
                    issue_loads(i + LA)
                hh = h[d_]
                if first:
                    P.memset(hh, 0.0)
                hb, sl, cl = h16[i % 2], sld[i % 4], cdl[i % 4]
                P.cp(hb, hh, eng="act")
                P.dma(P.blk(self.HF if d_ == 0 else self.HB, b, t), hb)
                P.tt(hh.re("p (h d) -> p h d", h=16), hh.re("p (h d) -> p h d", h=16),
                     cl[:, d_ * 16:(d_ + 1) * 16].re("p (h o) -> p h o", o=1).bc([128, 16, 64]), ALU.mult)
                P.tt(hh, hh, sl, ALU.add)
            return f

        steps = [mk(i) for i in range(len(seq))]
        return steps

    def prefetch_w_out(self, src):
        w, blk = self.A_top([128, 16, 1024], BF16, "w_out")
        self.load_w16(w, src, 1024)
        self.pref["w_out"] = (w, blk)

    def ssd_pass_b2(self, li):
        P = self
        I = self.I
        P.prefetch_w_out(I["ev_w_out"][li])
        nrow = P.A([128, 1024], F32, "nrow")
        P.bcast_rows(nrow, I["ev_ssd_norm"][li:li + 1, :])
        cts = [P.A([128, 2, 128], BF16, "ct%d" % i) for i in range(3)]
        hfs = [P.A([128, 1024], BF16, "hf%d" % i) for i in range(3)]
        hbs = [P.A([128, 1024], BF16, "hb%d" % i) for i in range(3)]
        ees = [P.A([128, 32], F32, "ee%d" % i) for i in range(3)]
        yds = [P.A([128, 1024], F32, "yd%d" % i) for i in range(3)]
        zs = [P.A([128, 1024], F32, "z%d" % i) for i in range(3)]
        t1s = [P.A([128, 1024], F32, "t1_%d" % i) for i in range(2)]
        t2s = [P.A([128, 1024], F32, "t2_%d" % i) for i in range(2)]
        sq = P.A([128, 1024], F32, "sq")
        ybs = [P.A([128, 1024], BF16, "yb%d" % i) for i in range(2)]
        sss = [P.A([128, 1], F32, "ss%d" % i) for i in range(2)]
        rstds = [P.A([128, 1], F32, "rstd%d" % i) for i in range(2)]
        yT = [P.A([128, 8, 128], BF16, "yT%d" % i) for i in range(2)]
        n = 0
        for b in range(NB):
            for t in range(NCH):
                ct, hf, hb, ee, yd, z = cts[n % 3], hfs[n % 3], hbs[n % 3], ees[n % 3], yds[n % 3], zs[n % 3]
                t1, t2, ss, rstd = t1s[n % 2], t2s[n % 2], sss[n % 2], rstds[n % 2]
                P.dma(ct.re("p a b -> p (a b)"), P.blk(self.CTs, b, t))
                P.dma(hf, P.blk(self.HF, b, t))
                P.dma(hb, P.blk(self.HB, b, t))
                P.dma(ee, P.blk(self.EE, b, t))
                P.dma(yd, P.blk(self.YD, b, t))
                P.dma(z, P.tm(self.Z, b, t * 128, t * 128 + 128))
                pairs = (0, 2, 6)
                pf, pb_ = pairs[(2 * n) % 3], pairs[(2 * n + 1) % 3]
                for g in range(2):
                    P.mm(P.PS(pf + g, 0, 512), ct[:, g, :], hf[:, g * 512:(g + 1) * 512])
                for g in range(2):
                    P.mm(P.PS(pb_ + g, 0, 512), ct[:, g, :], hb[:, g * 512:(g + 1) * 512])
                P.tt(t1.re("p (h d) -> p h d", h=16), P.PS(pf, 0, 1024).re("p (h d) -> p h d", h=16),
                     ee[:, 0:16].re("p (h o) -> p h o", o=1).bc([128, 16, 64]), ALU.mult)
                P.tt(t2.re("p (h d) -> p h d", h=16), P.PS(pb_, 0, 1024).re("p (h d) -> p h d", h=16),
                     ee[:, 16:32].re("p (h o) -> p h o", o=1).bc([128, 16, 64]), ALU.mult)
                P.tt(t1, t1, yd, ALU.add)
                P.tt(t1, t1, t2, ALU.add)
                P.tt(t1, t1, z, ALU.mult)
                P.act(sq, t1, AF.Square, accum=ss)
                P.act(rstd, ss, AF.Sqrt, bias=EPS, scale=1.0 / 1024)
                P.recip(rstd, rstd)
                yb = ybs[n % 2]
                P.stt(yb, t1, rstd[:, 0:1], nrow, ALU.mult, ALU.mult)
                yt = yT[n % 2]
                tb_ = P.PS16(4 + n % 2)
                for c in range(8):
                    P.tr(tb_[:, c * 128:(c + 1) * 128], yb[:, c * 128:(c + 1) * 128], self.ident16)
                P.evac(yt.re("p a b -> p (a b)"), tb_, n)
                P.dma(P.fm(self.MIX, b, t * 128, t * 128 + 128, 0, 1024), yt)
                n += 1
        self.stage_end()

    def ln_fm(self, src, tw, nch, bank):
        P = self
        sq = self._ln_sq
        sq16 = self._ln_sq16
        mean, rstd, m2 = self._ln_mean, self._ln_rstd, self._ln_m2
        ps_s = P.PS(bank, 0, tw)
        ps_q = P.PS(bank + 1, 0, tw)
        for c in range(nch):
            P.cp(sq16[c % 2][:, 0:tw], src[:, c, 0:tw], eng="pool")
            P.mm(ps_s, self.ones16, sq16[c % 2][:, 0:tw], start=(c == 0), stop=(c == nch - 1))
        for c in range(nch):
            P.act(sq[c % 2][:, 0:tw], src[:, c, 0:tw], AF.Square)
            P.mm(ps_q, self.ones16, sq[c % 2][:, 0:tw], start=(c == 0), stop=(c == nch - 1))
        inv = 1.0 / (nch * 128)
        P.act(mean[:, 0:tw], ps_s, AF.Copy, scale=inv)
        P.tt(m2[:, 0:tw], mean[:, 0:tw], mean[:, 0:tw], ALU.mult)
        P.stt(rstd[:, 0:tw], ps_q, inv, m2[:, 0:tw], ALU.mult, ALU.subtract)
        P.act(rstd[:, 0:tw], rstd[:, 0:tw], AF.Sqrt, bias=EPS)
        P.recip(rstd[:, 0:tw], rstd[:, 0:tw])
        return mean, rstd

    def ln_alloc(self):
        P = self
        self._ln_sq = [P.A([128, 512], BF16, "lnsq%d" % i) for i in range(2)]
        self._ln_sq16 = [P.A([128, 512], BF16, "lnsq16_%d" % i) for i in range(2)]
        self._ln_mean = P.A([128, 512], F32, "lnmean")
        self._ln_rstd = P.A([128, 512], F32, "lnrstd")
        self._ln_m2 = P.A([128, 512], F32, "lnm2")

    def conformer(self, li):
        P = self
        I = self.I
        cols = P.A([128, 8, 34], F32, "cvcols")
        P.rows_to_cols(cols, [I["ev_cv_w"][li], I["ev_cv_b"][li:li + 1, :], I["ev_cv_ln_g"][li:li + 1, :],
                              I["ev_cv_ln_b"][li:li + 1, :]], 1024)
        wc = cols
        cvb, cg, cb = cols[:, :, 31], cols[:, :, 32], cols[:, :, 33]
        diag = P.A([128, 248, 128], BF16, "diag31")
        for c in range(8):
            for k in range(31):
                P.ts(diag[:, c * 31 + k, :], self.ident16, wc[:, c, k:k + 1], ALU.mult)
        self.ln_alloc()
        wins = [P.A([128, 8, 542], BF16, "cwin%d" % i) for i in range(2)]
        gts = [P.A([128, 8, 512], F32, "gt0")] * 2
        hcs = [P.A([128, 8, 512], F32, "hc0")] * 2
        b1 = self.ssd_pass_b1_steps()
        tmp = [P.A([128, 512], F32, "ctmp%d" % i) for i in range(2)]
        so = [P.A([128, 8, 512], BF16, "cso0")] * 2
        n = 0
        ncb = 0
        for b in range(NB):
            for (t0, tw) in FM_TILES:
                win, gt, o_, hc = wins[n % 2], gts[n % 2], so[n % 2], hcs[n % 2]
                n += 1
                wv = win[:, :, 0:tw + 30]
                P.load_window(wv, self.U, b, t0, tw, 15)
                P.dma(gt[:, :, 0:tw], P.fm(self.G, b, t0, t0 + tw))
                for c in range(8):
                    o = P.PS((0, 1, 4, 5, 6, 7)[ncb % 6], 0, tw)
                    ncb += 1
                    for k in range(31):
                        P.mm(o, diag[:, c * 31 + k, :], wv[:, c, k:k + tw], start=(k == 0), stop=(k == 30))
                    P.act(hc[:, c, 0:tw], o, AF.Identity, bias=cvb[:, c:c + 1])
                    if b1:
                        b1.pop(0)()
                mean, rstd = P.ln_fm(hc, tw, 8, 2)
                for c in range(8):
                    tp = tmp[c % 2]
                    P.tt(tp[:, 0:tw], hc[:, c, 0:tw], mean[:, 0:tw], ALU.subtract)
                    P.tt(tp[:, 0:tw], tp[:, 0:tw], rstd[:, 0:tw], ALU.mult)
                    P.act(tp[:, 0:tw], tp[:, 0:tw], AF.Silu, bias=cb[:, c:c + 1], scale=cg[:, c:c + 1])
                    P.tt(o_[:, c, 0:tw], tp[:, 0:tw], gt[:, c, 0:tw], ALU.mult)
                P.dma(P.fm(self.MIX, b, t0, t0 + tw, 1024, 2048), o_[:, :, 0:tw])
        while b1:
            b1.pop(0)()
        self.stage_end()

    def stage_out(self, layer, last):
        P = self
        I = self.I
        src = (I["ev_w_out"] if layer % 2 == 0 else I["od_w_out"])[layer // 2]
        if "w_out" in self.pref:
            w, wblk = self.pref.pop("w_out")
        else:
            w, wblk = P.A_top([128, 16, 1024], BF16, "w_out")
            P.load_w16(w, src, 1024)
        self.ln_alloc()
        mixs = [P.A([128, 16, 512], BF16, "mix%d" % i) for i in range(2)]
        xts = [P.A([128, 8, 512], F32, "oxt0")] * 2
        rs_ = [P.A([128, 8, 512], F32, "r%d" % i) for i in range(2)]
        tmp = [P.A([128, 512], F32, "otmp%d" % i) for i in range(2)]
        xo = [P.A([128, 8, 512], F32, "xo%d" % i) for i in range(2)]
        tok = [P.A([128, 1024], F32, "tok%d" % i) for i in range(2)]
        n = 0
        nt = 0
        nob = 0
        tmp2 = [P.A([128, 512], F32, "otmpb%d" % i) for i in range(2)]
        for b in range(NB):
            for (t0, tw) in FM_TILES:
                if last and t0 < CTX:
                    continue
                i = self.cond_idx(b, t0)
                mix, xt, xo_, r = mixs[n % 2], xts[n % 2], xo[n % 2], rs_[n % 2]
                n += 1
                P.dma(mix[:, :, 0:tw], P.fm(self.MIX, b, t0, t0 + tw))
                P.dma(xt[:, :, 0:tw], P.fm(self.XT, b, t0, t0 + tw))
                obanks = (0, 1) if (last and not self.debug) else (0, 1, 4, 5, 6, 7)
                for oc in range(8):
                    o = P.PS(obanks[nob % len(obanks)], 0, tw)
                    nob += 1
                    for k in range(16):
                        P.mm(o, w[:, k, oc * 128:(oc + 1) * 128], mix[:, k, 0:tw], start=(k == 0), stop=(k == 15))
                    tp = tmp[oc % 2]
                    P.act(tp[:, 0:tw], xt[:, oc, 0:tw], AF.Copy, scale=ALPHA)
                    P.stt(r[:, oc, 0:tw], o, self.modT[:, 16 + oc, i:i + 1], tp[:, 0:tw], ALU.mult, ALU.add)
                mean, rstd = P.ln_fm(r, tw, 8, 2)
                for c in range(8):
                    tp = tmp2[c % 2]
                    P.tt(tp[:, 0:tw], r[:, c, 0:tw], mean[:, 0:tw], ALU.subtract, eng="pool")
                    P.tt(tp[:, 0:tw], tp[:, 0:tw], rstd[:, 0:tw], ALU.mult)
                    P.act(xo_[:, c, 0:tw], tp[:, 0:tw], AF.Identity, bias=self.lnb[:, c:c + 1], scale=self.lng[:, c:c + 1])
                if last and not self.debug:
                    for sub in range(tw // 128):
                        tk = tok[nt % 2]
                        pb = 4 + (nt % 2) * 2
                        nt += 1
                        for c in range(8):
                            P.tr(P.PS(pb, c * 128, 128), xo_[:, c, sub * 128:(sub + 1) * 128], self.ident)
                        P.evac(tk, P.PS(pb, 0, 1024), nt)
                        tok0 = t0 - CTX + sub * 128
                        op = P.dma(P._dv(V(self.out[b, tok0:tok0 + 128, :], [])), tk)
                        self.S.out_dmas.append(op)
                else:
                    P.dma(P.fm(self.XT, b, t0, t0 + tw), xo_[:, :, 0:tw])
        P.release_top(wblk)
        self.stage_end()

    def stage_final(self):
        P = self
        if not self.debug:
            return
        buf = [P.A([128, 8, 512], F32, "dbg%d" % i) for i in range(2)]
        n = 0
        for b in range(NB):
            for (t0, tw) in FM_TILES:
                bb = buf[n % 2]
                n += 1
                P.dma(bb[:, :, 0:tw], P.fm(self.XT, b, t0, t0 + tw))
                op = P.dma(V(self.out[b, :, t0:t0 + tw].rearrange("(c p) t -> p c t", p=128), []), bb[:, :, 0:tw])
                self.S.out_dmas.append(op)

    def load_wa(self, li):
        P = self
        I = self.I
        wa, blk = P.A_top([128, 8, 3584], BF16, "w_in_a")
        for k in range(8):
            r = I["od_w_in"][li, k * 128:(k + 1) * 128, :]
            P.dma(wa[:, k, 0:1024], P.IN(r[:, 0:1024]), eng="pool")
            P.dma(wa[:, k, 1024:2048], P.IN(r[:, 2048:3072]), eng="pool")
            P.dma(wa[:, k, 2048:3072], P.IN(r[:, 3072:4096]), eng="pool")
            for kv in range(4):
                P.dma(wa[:, k, 3072 + kv * 128:3072 + (kv + 1) * 128].re("p (o d) -> p o d", o=2),
                      P.IN(r[:, 4096 + kv * 64:4096 + (kv + 1) * 64].rearrange("p (o d) -> p o d", o=1).broadcast_to([128, 2, 64])), eng="pool")
        return wa, blk

    def load_wb(self, li):
        P = self
        I = self.I
        wb, blk = P.A_top([128, 8, 2304], BF16, "w_in_b")
        for k in range(8):
            r = I["od_w_in"][li, k * 128:(k + 1) * 128, :]
            P.dma(wb[:, k, 0:1024], P.IN(r[:, 1024:2048]), eng="pool")
            P.dma(wb[:, k, 1024:1280], P.IN(r[:, 4352:4608]), eng="pool")
            P.dma(wb[:, k, 1280:2304], P.IN(r[:, 4608:5632]), eng="pool")
        return wb, blk

    def odd_layer(self, li, layer, last):
        P = self
        I = self.I
        wa, wablk = self.pref.pop("wa")
        self.pref["wb"] = self.load_wb(li)
        cos = P.A([128, 512], F32, "cos")
        sin = P.A([128, 512], F32, "sin")
        xt = P.A([128, 8, 512], F32, "xt0")
        hTs = [P.A([128, 8, 512], BF16, "hT%d" % i) for i in range(2)]
        t_a = [P.A([128, 512], F32, "ta%d" % i) for i in range(2)]
        t_b = [P.A([128, 512], F32, "tb%d" % i) for i in range(2)]
        gu = P.A([128, 8, 512], F32, "gu")
        q16 = [P.A([128, 512], BF16, "q16_%d" % i) for i in range(2)]
        sto = [P.A([128, 512], F32, "sto%d" % i) for i in range(4)]
        sto16 = [P.A([128, 512], BF16, "sto16_%d" % i) for i in range(4)]
        n = 0
        nb = 0
        ns = 0

        def proj_fm(lhs_fn, hT, tw, bank):
            o = P.PS(bank, 0, tw)
            for k in range(8):
                P.mm(o, lhs_fn(k), hT[:, k, 0:tw], start=(k == 0), stop=(k == 7))
            return o

        def rope_store(o, dst, tw, t0, j):
            if t0 < CTX:
                P.evac(dst, o, j)
                return
            qb = q16[j % 2]
            P.cp(qb[:, 0:tw], o, eng="act")
            sw = P.PS(4 + (j % 2), 0, tw)
            P.mm(sw, self.perm16, qb[:, 0:tw])
            ta, tb = t_a[j % 2], t_b[j % 2]
            P.tt(ta[:, 0:tw], o, cos[:, 0:tw], ALU.mult)
            P.tt(tb[:, 0:tw], sw, sin[:, 0:tw], ALU.mult)
            P.tt(dst, ta[:, 0:tw], tb[:, 0:tw], ALU.add)

        for b in range(NB):
            for (t0, tw) in FM_TILES:
                hT = hTs[n % 2]
                n += 1
                P.load_hT(b, t0, tw, xt, hT)
                if t0 >= CTX:
                    P.dma(cos[:, 0:tw], P.IN(I["k_cos"][:, t0 - CTX:t0 - CTX + tw]))
                    P.dma(sin[:, 0:tw], P.IN(I["k_sin"][:, t0 - CTX:t0 - CTX + tw]))
                for j in range(8):
                    ou = proj_fm(lambda k: wa[:, k, j * 128:(j + 1) * 128], hT, tw, nb % 4)
                    nb += 1
                    P.act(gu[:, j, 0:tw], ou, AF.Gelu_apprx_tanh)
                for j in range(8):
                    og = proj_fm(lambda k: wa[:, k, 1024 + j * 128:1024 + (j + 1) * 128], hT, tw, nb % 4)
                    nb += 1
                    tb = t_b[j % 2]
                    s_ = sto[ns % 4]
                    ns += 1
                    P.act(tb[:, 0:tw], og, AF.Silu)
                    P.tt(s_[:, 0:tw], gu[:, j, 0:tw], tb[:, 0:tw], ALU.mult)
                    P.dma(P.fm(self.AT, b, t0, t0 + tw, j * 128, j * 128 + 128)[:, 0, :], s_[:, 0:tw])
                for j in range(8):
                    o = proj_fm(lambda k: wa[:, k, 2048 + j * 128:2048 + (j + 1) * 128], hT, tw, nb % 4)
                    nb += 1
                    s_ = sto16[ns % 4]
                    ns += 1
                    rope_store(o, s_[:, 0:tw], tw, t0, j)
                    P.dma(P.fm(self.QT, b, t0, t0 + tw, j * 128, j * 128 + 128)[:, 0, :], s_[:, 0:tw])
                for kv in range(4):
                    o = proj_fm(lambda k: wa[:, k, 3072 + kv * 128:3072 + (kv + 1) * 128], hT, tw, nb % 4)
                    nb += 1
                    s_ = sto16[ns % 4]
                    ns += 1
                    rope_store(o, s_[:, 0:tw], tw, t0, kv)
                    P.dma(P.fm(self.KT, b, t0, t0 + tw, kv * 128, kv * 128 + 128)[:, 0, :], s_[:, 0:tw])
        P.release_top(wablk)
        self.stage_end()

        wb, wbblk = self.pref.pop("wb")
        lgrow = P.A([128, 1024], F32, "lgrow")
        lbrow = P.A([128, 1024], F32, "lbrow")
        bsrow = P.A([128, 1024], F32, "bsrow")
        P.bcast_rows(lgrow, I["od_mlp_ln_g"][li:li + 1, :])
        P.bcast_rows(lbrow, I["od_mlp_ln_b"][li:li + 1, :])
        P.bcast_rows(bsrow, I["od_bs"][li:li + 1, :])
        wsT = P.A([128, 8, 128], BF16, "wsT")
        wstmp = [P.A([128, 128], F32, "wstmp%d" % i) for i in range(2)]
        for g in range(8):
            P.dma(wstmp[g % 2], P.IN(I["od_ws"][li, g]))
            P.tr(P.PS(g % 2, 0, 128), wstmp[g % 2], self.ident)
            P.evac(wsT[:, g, :], P.PS(g % 2, 0, 128), g)
        xt = P.A([128, 8, 512], F32, "xt0")
        hTs = [P.A([128, 8, 512], BF16, "hT%d" % i) for i in range(2)]
        ats = [P.A([128, 8, 128], F32, "at%d" % i) for i in range(2)]
        vfs = [P.A([128, 1024], F32, "vf%d" % i) for i in range(2)]
        vns = [P.A([128, 1024], BF16, "vn%d" % i) for i in range(2)]
        sts = [P.A([128, 2, 6], F32, "bnst%d" % i) for i in range(2)]
        mvs = [P.A([128, 2], F32, "bnmv%d" % i) for i in range(2)]
        rss = [P.A([128, 1], F32, "bnrs%d" % i) for i in range(2)]
        mxos = [P.A([128, 1024], F32, "mxo0")] * 2
        smx = [P.A([128, 8, 128], BF16, "smx%d" % i) for i in range(2)]
        sva = [P.A([128, 256], BF16, "sva%d" % i) for i in range(2)]
        sgd = [P.A([128, 1024], F32, "sgd%d" % i) for i in range(2)]
        n = 0
        cnt = {"bank": 0}

        def nbank():
            cnt["bank"] += 1
            return (cnt["bank"] - 1) % 6

        def part_a(b, tok0, hT, sub, nz):
            hs = lambda k: hT[:, k, sub * 128:(sub + 1) * 128]
            at = ats[nz % 2]
            vf, vn, st, mv, rs = vfs[nz % 2], vns[nz % 2], sts[nz % 2], mvs[nz % 2], rss[nz % 2]
            P.dma(at, P.fm(self.AT, b, tok0, tok0 + 128))
            for half in range(2):
                o = P.PS(nbank(), 0, 512)
                for k in range(8):
                    P.mm(o, hs(k), wb[:, k, half * 512:(half + 1) * 512], start=(k == 0), stop=(k == 7))
                P.act(vf[:, half * 512:(half + 1) * 512], o, AF.Gelu_apprx_tanh)
                h0 = half * 512
                stv, vfv = st.ap[:, half, :], vf.ap[:, h0:h0 + 512]
                self.S.add("dve", lambda e, stv=stv, vfv=vfv: e.bn_stats(out=stv, in_=vfv), reads=[vf], writes=[st])
            sta, mva = st.ap.rearrange("p a b -> p (a b)"), mv.ap
            self.S.add("dve", lambda e, sta=sta, mva=mva: e.bn_aggr(out=mva, in_=sta), reads=[st], writes=[mv])
            P.act(rs, mv[:, 1:2], AF.Sqrt, bias=EPS)
            P.recip(rs, rs)
            P.ts(vf, vf, mv[:, 0:1], ALU.subtract, rs[:, 0:1], ALU.mult)
            P.tt(vf, vf, lgrow, ALU.mult)
            P.tt(vn, vf, lbrow, ALU.add)
            o = P.PS(nbank(), 0, 256)
            for k in range(8):
                P.mm(o, hs(k), wb[:, k, 1024:1280], start=(k == 0), stop=(k == 7))
            sv = sva[nz % 2]
            P.cp(sv, o, eng="act")
            P.dma(P.tm(self.VA, b, tok0, tok0 + 128), sv)
            sg = sgd[nz % 2]
            for half in range(2):
                o = P.PS(nbank(), 0, 512)
                for k in range(8):
                    P.mm(o, hs(k), wb[:, k, 1280 + half * 512:1280 + (half + 1) * 512], start=(k == 0), stop=(k == 7))
                P.act(sg[:, half * 512:(half + 1) * 512], o, AF.Silu)
            P.dma(P.tm(self.GD, b, tok0, tok0 + 128), sg)

        def part_b(b, tok0, nz):
            at, vn, mxo = ats[nz % 2], vns[nz % 2], mxos[nz % 2]
            for g in range(8):
                P.mm(P.PS(6 + g // 4, (g % 4) * 128, 128), vn[:, g * 128:(g + 1) * 128], wsT[:, g, :])
            P.tt(mxo, P.PS(6, 0, 1024), bsrow, ALU.add)
            sm = smx[nz % 2]
            P.tt(sm, mxo.re("p (g q) -> p g q", g=8), at, ALU.mult)
            P.dma(P.fm(self.MIX, b, tok0, tok0 + 128, 0, 1024), sm)

        work = []
        for b in range(NB):
            for (t0, tw) in FM_TILES:
                for sub in range(tw // 128):
                    work.append((b, t0, tw, sub))
        cur_hT = None
        pend = None
        for nz, (b, t0, tw, sub) in enumerate(work):
            if sub == 0:
                cur_hT = hTs[n % 2]
                n += 1
                P.load_hT(b, t0, tw, xt, cur_hT)
            tok0 = t0 + sub * 128
            part_a(b, tok0, cur_hT, sub, nz)
            if pend is not None:
                part_b(*pend)
            pend = (b, tok0, nz)
        part_b(*pend)
        P.release_top(wbblk)
        self.stage_end()
        self.attention(li, last)

    def attention(self, li, last):
        P = self
        I = self.I
        P.prefetch_w_out(I["od_w_out"][li])
        esink = P.A([128, 16], F32, "esink")
        P.bcast_rows(esink, I["od_sink"][li:li + 1, :])
        P.act(esink, esink, AF.Exp)
        kc = P.A([128, 4, 256], BF16, "kctx")
        vc = P.A([128, 2, 4, 65], BF16, "vctx")
        qts = [P.A([128, 8, 128], BF16, "qt%d" % i) for i in range(2)]
        qms = [P.A([128, 16, 128], BF16, "qm%d" % i) for i in range(2)]
        kws = [P.A([128, 4, 384], BF16, "kw%d" % i) for i in range(2)]
        vws = [P.A([128, 3, 4, 65], BF16, "vw%d" % i) for i in range(2)]
        gds = [P.A([128, 1024], F32, "gd%d" % i) for i in range(2)]
        den = P.A([128, 16], F32, "den")
        yd = P.A([128, 1024], F32, "yd")
        yT = [P.A([128, 8, 128], BF16, "ayT%d" % i) for i in range(2)]
        for t_ in vws + [vc]:
            P.memset(t_, 1.0)
        E = [P.A([128, 5, 512], BF16, "E%d" % i) for i in range(3)]
        cnt = {"n": 0, "ns": 0, "ne": 0, "ev": 0}

        def prep(b, t):
            n = cnt["n"]
            cnt["n"] += 1
            qt, kw, vw, gd, qm = qts[n % 2], kws[n % 2], vws[n % 2], gds[n % 2], qms[n % 2]
            tok0 = t * 128
            P.dma(qt, P.fm(self.QT, b, tok0, tok0 + 128))
            P.dma(gd, P.tm(self.GD, b, tok0, tok0 + 128))
            for e_ in range(2):
                P.ts(qm.re("p (j e) q -> p j e q", e=2)[:, :, e_, :], qt, self.hmask[:, e_:e_ + 1], ALU.mult)
            kbs = []
            if t >= 2:
                lo = max(t - 1, 2)
                hi = min(t + 1, NCH - 1)
                P.dma(kw[:, :, 0:(hi - lo + 1) * 128], P.fm(self.KT, b, lo * 128, (hi + 1) * 128))
                for j, tt_ in enumerate(range(lo, hi + 1)):
                    P.dma(vw[:, j, :, 0:64], P.tm(self.VA, b, tt_ * 128, (tt_ + 1) * 128).re("p (h d) -> p h d", h=4))
                    mask = None
                    if tt_ == t - 1:
                        mask = self.negb16
                    elif tt_ == t + 1:
                        mask = self.negf16
                    kbs.append((kw, vw, j, mask))
            kbs.append((kc, vc, 0, None))
            kbs.append((kc, vc, 1, None))
            return dict(b=b, t=t, qm=qm, gd=gd, kbs=kbs, n=n, E={})

        def st(c, kv):
            Et = E[cnt["ne"] % 3]
            cnt["ne"] += 1
            c["E"][kv] = Et
            for jj, (ksrc, vsrc, j, mask) in enumerate(c["kbs"]):
                sp_ = P.PS(cnt["ns"] % 4, 0, 512)
                cnt["ns"] += 1
                if mask is not None:
                    P.mm(sp_, self.ident16, mask, start=True, stop=False)
                P.mm(sp_, ksrc[:, kv, j * 128:(j + 1) * 128],
                     c["qm"][:, kv * 4:(kv + 1) * 4, :].re("p h q -> p (h q)"), start=(mask is None), stop=True)
                P.act(Et[:, jj, :], sp_, AF.Exp, scale=ATT_SCALE)

        def pv(c, kv):
            Et = c["E"][kv]
            nk = len(c["kbs"])
            for r in range(4):
                h = kv * 4 + r
                o = P.PS(4 + h // 7, (h % 7) * 65, 65)
                for jj, (ksrc, vsrc, j, mask) in enumerate(c["kbs"]):
                    P.mm(o, Et[:, jj, r * 128:(r + 1) * 128], vsrc[:, j, kv, :], start=(jj == 0), stop=(jj == nk - 1))

        def fin(c):
            gd = c["gd"]
            for bank, h0, nh in ((4, 0, 7), (5, 7, 7), (6, 14, 2)):
                ov = P.PS(bank, 0, nh * 65).re("p (h d) -> p h d", d=65)
                P.tt(den[:, h0:h0 + nh].re("p (h o) -> p h o", o=1), ov[:, :, 64:65],
                     esink[:, h0:h0 + nh].re("p (h o) -> p h o", o=1), ALU.add)
            P.recip(den, den)
            for bank, h0, nh in ((4, 0, 7), (5, 7, 7), (6, 14, 2)):
                ov = P.PS(bank, 0, nh * 65).re("p (h d) -> p h d", d=65)
                P.tt(yd[:, h0 * 64:(h0 + nh) * 64].re("p (h d) -> p h d", d=64), ov[:, :, 0:64],
                     den[:, h0:h0 + nh].re("p (h o) -> p h o", o=1).bc([128, nh, 64]), ALU.mult)
            ydg = ydgs[c["n"] % 2]
            P.tt(ydg, yd, gd, ALU.mult, eng="pool")
            yt = yT[c["n"] % 2]
            tb_ = P.PS16(7)
            for cc in range(8):
                P.tr(tb_[:, cc * 128:(cc + 1) * 128], ydg[:, cc * 128:(cc + 1) * 128], self.ident16)
            P.evac(yt.re("p a b -> p (a b)"), tb_, cnt["ev"])
            cnt["ev"] += 1
            tok0 = c["t"] * 128
            P.dma(P.fm(self.MIX, c["b"], tok0, tok0 + 128, 1024, 2048), yt)

        ydgs = [P.A([128, 1024], BF16, "ydg%d" % i) for i in range(2)]
        for b in range(NB):
            P.dma(kc, P.fm(self.KT, b, 0, CTX))
            for j in range(2):
                P.dma(vc[:, j, :, 0:64], P.tm(self.VA, b, j * 128, (j + 1) * 128).re("p (h d) -> p h d", h=4))
            qblocks = list(range(2, NCH)) + ([] if last else [0, 1])
            items = []
            for t in qblocks:
                for kv in range(4):
                    items.append((t, kv))
            ctxs = {}
            ctxs[items[0][0]] = prep(b, items[0][0])
            st(ctxs[items[0][0]], 0)
            for i_, (t, kv) in enumerate(items):
                if i_ + 1 < len(items):
                    t2, kv2 = items[i_ + 1]
                    if t2 not in ctxs:
                        ctxs[t2] = prep(b, t2)
                    st(ctxs[t2], kv2)
                pv(ctxs[t], kv)
                if kv == 3:
                    fin(ctxs[t])
        self.stage_end()


def _consts():
    i = np.arange(128)
    tri = (i[:, None] <= i[None, :]).astype(np.float32)
    trirev = (i[:, None] >= i[None, :]).astype(np.float32)
    negf = np.where(i[None, :] >= i[:, None], 0.0, NEG).astype(np.float32)
    negb = np.where(i[None, :] <= i[:, None], 0.0, NEG).astype(np.float32)
    perm = np.zeros((128, 128), np.float32)
    for po in range(128):
        d = po % 64
        if d < 32:
            perm[po + 32, po] = -1.0
        else:
            perm[po - 32, po] = 1.0
    t = np.arange(SEQ)
    row = (t // 64).astype(np.float32)
    col = (t % 64).astype(np.float32)
    inv = (10000.0 ** (-np.arange(16, dtype=np.float32) / 16)).astype(np.float32)
    ang = np.concatenate([row[:, None] * inv, col[:, None] * inv], -1).astype(np.float32)
    p = np.arange(128)
    cosT = np.cos(ang).astype(np.float32)[:, (p % 64) % 32].T.copy()
    sinT = np.sin(ang).astype(np.float32)[:, (p % 64) % 32].T.copy()
    hmask = np.zeros((128, 2), np.float32)
    hmask[:64, 0] = 1.0
    hmask[64:, 1] = 1.0
    return dict(k_hmask=hmask, k_ident=np.eye(128, dtype=np.float32), k_tri=tri, k_trirev=trirev,
                k_negf=np.tile(negf, (1, 4)), k_negb=np.tile(negb, (1, 4)), k_perm=perm,
                k_cos=np.ascontiguousarray(cosT), k_sin=np.ascontiguousarray(sinT))


_CACHE = {}


def _get_nc(n_layers=DEPTH, debug=False):
    key = (n_layers, debug)
    if key not in _CACHE:
        _CACHE[key] = Prog(n_layers, debug).build()
    return _CACHE[key]


def make_in_maps(inputs, n_cores):
    f = lambda a: np.ascontiguousarray(np.asarray(a, dtype=np.float32))
    shared = {}
    for k, v in inputs.items():
        if k in ("x", "c", "ctx"):
            continue
        v = f(v)
        if k == "c_ctx":
            v = v.reshape(1, D)
        elif k == "ev_a_log":
            v = v.reshape(2, 32)
        elif k == "od_bs":
            v = v.reshape(2, 1024)
        shared[k] = v
    shared.update(_consts())
    x, c, ctx = f(inputs["x"]), f(inputs["c"]), f(inputs["ctx"])
    maps = []
    for i in range(n_cores):
        m = dict(shared)
        m["x"] = x[i * NB:(i + 1) * NB]
        m["c"] = c[i * NB:(i + 1) * NB]
        m["ctx"] = ctx[i * NB:(i + 1) * NB]
        maps.append(m)
    return maps


def kernel(**inputs):
    nc = _get_nc()
    n_cores = 8
    maps = make_in_maps(inputs, n_cores)
    res = run_bass_kernel_spmd(nc, maps, core_ids=list(range(n_cores)))
    return np.concatenate([r["out"] for r in res.results], axis=0).astype(np.float32)
```

```python
import math
from contextlib import ExitStack
import numpy as np
import concourse.bass as bass
import concourse.mybir as mybir
from concourse.bass_utils import run_bass_kernel_spmd

F32 = mybir.dt.float32
BF16 = mybir.dt.bfloat16
AF = mybir.ActivationFunctionType
ALU = mybir.AluOpType

D = 1024
DEPTH = 4
NB = 2
CTX = 256
SEQ = 2048
NT = CTX + SEQ
NCH = NT // 128
EVEN_IN = 5664
ODD_IN = 5632
ALPHA = (2 * DEPTH) ** 0.25
EPS = 1e-5
ATT_SCALE = 64 ** -0.5
NEG = -30000.0
FM_TILES = [(0, 256)] + [(256 + 512 * i, 512) for i in range(4)]


class Buf:
    __slots__ = ("name", "w", "r", "excl")

    def __init__(self, name="", excl=False):
        self.name = name
        self.w = None
        self.r = []
        self.excl = excl


class V:
    __slots__ = ("ap", "b")

    def __init__(self, ap, b):
        self.ap = ap
        self.b = b

    def __getitem__(self, k):
        return V(self.ap[k], self.b)

    def bc(self, shape):
        return V(self.ap.broadcast_to(shape), self.b)

    def re(self, s, **kw):
        return V(self.ap.rearrange(s, **kw), self.b)


class Op:
    __slots__ = ("eng", "fn", "deps", "signal", "val", "sem", "dma", "idx", "snap", "waits")

    def __init__(self, eng, fn, dma):
        self.idx = 0
        self.snap = None
        self.waits = None
        self.eng = eng
        self.fn = fn
        self.deps = []
        self.signal = False
        self.val = 0
        self.sem = None
        self.dma = dma


ENGS = ("pe", "act", "dve", "pool", "sp")


class Sched:
    def __init__(self, nc, n_dma_sems=28):
        self.nc = nc
        self.ops = {e: [] for e in ENGS}
        self.n_dma_sems = n_dma_sems
        self.dma_rr = {"sp": 0, "pool": 0, "act": 0}
        self.dma_last = {}
        self.out_dmas = []

    def add(self, eng, fn, reads=(), writes=(), dma=False):
        op = Op(eng, fn, dma)
        self.nrec = getattr(self, "nrec", 0) + 1
        op.idx = self.nrec
        deps = set()
        R = [b for v in reads for b in v.b]
        W = [b for v in writes for b in v.b]
        for b in R:
            if b.w is not None:
                deps.add(b.w)
            if b.excl:
                for r_ in b.r:
                    if r_.eng != eng:
                        deps.add(r_)
        for b in W:
            if b.w is not None:
                deps.add(b.w)
            deps.update(b.r)
        for d in deps:
            if d.eng == "pe" and eng == "pe" and not d.dma and not dma:
                continue
            op.deps.append(d)
            d.signal = True
        if dma:
            k = (eng, self.dma_rr[eng] % self.n_dma_sems)
            self.dma_rr[eng] += 1
            prev = self.dma_last.get(k)
            if prev is not None:
                op.deps.append(prev)
            self.dma_last[k] = op
            op.sem = k
            op.signal = True
        for b in W:
            b.w = op
            b.r = []
        Ws = set(id(b) for b in W)
        for b in R:
            if id(b) not in Ws:
                if not dma:
                    b.r = [x for x in b.r if x.dma or x.eng != eng]
                b.r.append(op)
        self.ops[eng].append(op)
        return op

    def barrier(self):
        lasts = []
        for e in ENGS:
            for op in reversed(self.ops[e]):
                if not op.dma and op.fn is not None:
                    lasts.append(op)
                    break
        lasts += list(self.dma_last.values())
        for e in ENGS:
            op = Op(e, None, False)
            for d in lasts:
                if d.eng == e and not d.dma:
                    continue
                op.deps.append(d)
                d.signal = True
            self.ops[e].append(op)

    def emit(self):
        nc = self.nc
        with ExitStack() as es:
            sems = {}
            for e in ENGS:
                sems[e] = es.enter_context(nc.semaphore("s_" + e))
            for e in ("sp", "pool"):
                for i in range(self.n_dma_sems):
                    sems[(e, i)] = es.enter_context(nc.semaphore("d_%s_%d" % (e, i)))
            cnt = {k: 0 for k in sems}
            for e in ENGS:
                for op in self.ops[e]:
                    if op.dma:
                        cnt[op.sem] += 16
                        op.val = cnt[op.sem]
                    elif op.signal:
                        op.sem = e
                        cnt[e] += 1
                        op.val = cnt[e]
            self.maxvals = cnt
            allops = sorted((op for e in ENGS for op in self.ops[e]), key=lambda o: o.idx)
            kn = {e: {} for e in ENGS}
            for op in allops:
                k = kn[op.eng]
                need = {}
                src = {}
                for d in op.deps:
                    if need.get(d.sem, 0) < d.val:
                        need[d.sem] = d.val
                        src[d.sem] = d
                op.waits = []
                for s_, v_ in sorted(need.items(), key=lambda kv: -src[kv[0]].idx):
                    if k.get(s_, 0) >= v_:
                        continue
                    op.waits.append((s_, v_))
                    k[s_] = v_
                    sn = src[s_].snap
                    if sn:
                        for s2, v2 in sn.items():
                            if k.get(s2, 0) < v2:
                                k[s2] = v2
                if op.signal or op.dma:
                    k2 = dict(k)
                    op.snap = k2
            block = es.enter_context(nc.Block())
            finals = self.out_dmas

            def run(e, engobj):
                known = {}
                for op in self.ops[e]:
                    todo = list(op.waits)
                    attach = None
                    if todo and op.fn is not None and not op.dma and e in ("pe", "act", "dve"):
                        attach = todo.pop()
                    for s, v in todo:
                        engobj.wait_ge(sems[s], v)
                        known[s] = v
                    if op.fn is None:
                        continue
                    ins = op.fn(engobj)
                    if attach is not None:
                        ins._wait_ge(sems[attach[0]], attach[1])
                        known[attach[0]] = attach[1]
                    if op.dma:
                        ins.then_inc(sems[op.sem], 16)
                    elif op.signal:
                        ins.then_inc(sems[op.sem], 1)
                if e == "sp":
                    for op in finals:
                        if known.get(op.sem, 0) < op.val:
                            engobj.wait_ge(sems[op.sem], op.val)
                            known[op.sem] = op.val

            @block.sync
            def _(e):
                run("sp", e)

            @block.tensor
            def _(e):
                run("pe", e)

            @block.scalar
            def _(e):
                run("act", e)

            @block.vector
            def _(e):
                run("dve", e)

            @block.gpsimd
            def _(e):
                run("pool", e)


class Prog:
    def __init__(self, n_layers=DEPTH, debug=False, stop_after=0, use_barrier=False):
        self.stop_after = stop_after
        self.use_barrier = use_barrier
        self.n_layers = n_layers
        self.debug = debug
        self.nc = bass.Bass("TRN2", target_bir_lowering=False)
        self.S = Sched(self.nc)
        self.es = ExitStack()
        self.dram_bufs = {}
        self.dram_vs = {}
        self.uid = 0

    def dram_in(self, name, shape):
        return self.nc.dram_tensor(name, list(shape), F32, kind="ExternalInput").ap()

    def setup_mem(self):
        nc = self.nc
        ARENA_F32 = 46080
        self.arena = self.es.enter_context(nc.sbuf_tensor("arena", [128, ARENA_F32], F32))
        self.arena_sz = ARENA_F32
        self.arena_off = 0
        self.top_blocks = []
        self.live_tiles = []
        self.pref = {}
        self.pers = self.es.enter_context(nc.sbuf_tensor("pers", [128, 3 * 1024], F32))
        self.pers_off = 0
        self.psum = self.es.enter_context(nc.psum_tensor("psum", [128, 4096], F32))
        self.pbank = [Buf("bank%d" % i, excl=True) for i in range(8)]

    def _carve(self, base, off, shape, dt):
        n = int(np.prod(shape[1:]))
        words = (n + 1) // 2 if dt == BF16 else n
        words = (words + 7) // 8 * 8
        if dt == BF16:
            ap = base[0:shape[0], off:off + words].bitcast(BF16)[:, 0:n]
        else:
            ap = base[0:shape[0], off:off + n]
        if len(shape) == 3:
            ap = ap.rearrange("p (a b) -> p a b", a=shape[1])
        elif len(shape) == 4:
            ap = ap.rearrange("p (a b c) -> p a b c", a=shape[1], b=shape[2])
        return ap, words

    def _inherit(self, off, words, name):
        nb = Buf(name)
        keep = []
        for (o, w_, ob) in self.live_tiles:
            if o < off + words and off < o + w_:
                if ob.w is not None:
                    nb.r.append(ob.w)
                nb.r.extend(ob.r)
                if not (off <= o and o + w_ <= off + words):
                    keep.append((o, w_, ob))
            else:
                keep.append((o, w_, ob))
        keep.append((off, words, nb))
        self.live_tiles = keep
        return nb

    def A(self, shape, dt=F32, name="t"):
        off = self.arena_off
        ap, words = self._carve(self.arena, off, shape, dt)
        self.arena_off += words
        lim = min([o for (o, w_) in self.top_blocks], default=self.arena_sz)
        assert self.arena_off <= lim, ("arena overflow", name, self.arena_off, lim)
        return V(ap, [self._inherit(off, words, name)])

    def A_top(self, shape, dt=F32, name="t"):
        n = int(np.prod(shape[1:]))
        words = ((n + 1) // 2 if dt == BF16 else n)
        words = (words + 7) // 8 * 8
        low = min([o for (o, w_) in self.top_blocks], default=self.arena_sz)
        off = low - words
        assert off >= self.arena_off, ("arena top overflow", name, off, self.arena_off)
        ap, _ = self._carve(self.arena, off, shape, dt)
        blk = (off, words)
        self.top_blocks.append(blk)
        return V(ap, [self._inherit(off, words, name)]), blk

    def release_top(self, blk):
        self.top_blocks.remove(blk)

    def Pm(self, shape, dt=F32, name="p"):
        ap, words = self._carve(self.pers, self.pers_off, shape, dt)
        self.pers_off += words
        assert self.pers_off <= 3 * 1024, ("pers overflow", name, self.pers_off)
        return V(ap, [Buf(name)])

    def stage_end(self):
        if self.use_barrier:
            self.S.barrier()
        self.arena_off = 0
        self._rows = None
        self.nstage = getattr(self, "nstage", 0) + 1
        if self.stop_after and self.nstage >= self.stop_after:
            raise StopIteration

    def PS(self, bank, off=0, n=512):
        a = bank * 512 + off
        b0 = a // 512
        b1 = (a + n - 1) // 512
        return V(self.psum[:, a:a + n], [self.pbank[i] for i in range(b0, b1 + 1)])

    def PS16(self, bank):
        return V(self.psum[:, bank * 512:(bank + 1) * 512].bitcast(BF16), [self.pbank[bank]])

    def scratch(self, name, shape, dt):
        h = self.nc.dram_tensor(name, list(shape), dt, kind="Internal").ap()
        self.dram_bufs[name] = {}
        return (name, h)

    def _db(self, name, b, t0, t1):
        d = self.dram_bufs[name]
        out = []
        for blk in range(t0 // 128, (t1 + 127) // 128):
            k = (b, blk)
            if k not in d:
                d[k] = Buf("%s_%d_%d" % (name, b, blk))
            out.append(d[k])
        return out

    def fm(self, sc, b, t0, t1, r0=0, r1=None):
        name, h = sc
        if r1 is None:
            r1 = h.shape[1]
        ap = h[b, r0:r1, t0:t1].rearrange("(c p) t -> p c t", p=128)
        return self._dv(V(ap, self._db(name, b, t0, t1)))

    def tm(self, sc, b, t0, t1, c0=0, c1=None):
        name, h = sc
        if c1 is None:
            c1 = h.shape[2]
        return self._dv(V(h[b, t0:t1, c0:c1], self._db(name, b, t0, t1)))

    def blk(self, sc, b, t):
        name, h = sc
        return self._dv(V(h[b, t], self._db(name, b, t * 128, t * 128 + 128)))

    def _dv(self, v):
        self.dram_vs[id(v.b)] = v.b
        return v

    def dma(self, out, in_, eng=None):
        o, i = out.ap, in_.ap
        if eng is None:
            eng = "pool" if id(out.b) in self.dram_vs else "sp"
        return self.S.add(eng, lambda e: e.dma_start(out=o, in_=i), reads=[in_], writes=[out], dma=True)

    def mm(self, out, lhsT, rhs, start=True, stop=True):
        o, l, r = out.ap, lhsT.ap, rhs.ap
        return self.S.add("pe", lambda e: e.matmul(o, lhsT=l, rhs=r, start=start, stop=stop),
                          reads=[lhsT, rhs], writes=[out])

    def tr(self, out, in_, ident):
        o, i, d = out.ap, in_.ap, ident.ap
        return self.S.add("pe", lambda e: e.transpose(o, i, d), reads=[in_, ident], writes=[out])

    def act(self, out, in_, func, bias=None, scale=None, accum=None):
        o, i = out.ap, in_.ap
        kw = {}
        reads = [in_]
        if bias is not None:
            if isinstance(bias, V):
                kw["bias"] = bias.ap
                reads.append(bias)
            else:
                kw["bias"] = float(bias)
        if scale is not None:
            if isinstance(scale, V):
                kw["scale"] = scale.ap
                reads.append(scale)
            else:
                kw["scale"] = float(scale)
        writes = [out]
        if accum is not None:
            kw["accum_out"] = accum.ap
            writes.append(accum)
        return self.S.add("act", lambda e: e.activation(out=o, in_=i, func=func, **kw), reads=reads, writes=writes)

    def tt(self, out, in0, in1, op, eng="dve"):
        o, a, b = out.ap, in0.ap, in1.ap
        return self.S.add(eng, lambda e: e.tensor_tensor(out=o, in0=a, in1=b, op=op), reads=[in0, in1], writes=[out])

    def ts(self, out, in0, s1, op0, s2=None, op1=None, eng="dve"):
        o, a = out.ap, in0.ap
        reads = [in0]
        x1 = s1
        if isinstance(s1, V):
            reads.append(s1)
            x1 = s1.ap
        x2 = s2
        if isinstance(s2, V):
            reads.append(s2)
            x2 = s2.ap
        if op1 is None:
            return self.S.add(eng, lambda e: e.tensor_scalar(out=o, in0=a, scalar1=x1, scalar2=None, op0=op0),
                              reads=reads, writes=[out])
        return self.S.add(eng, lambda e: e.tensor_scalar(out=o, in0=a, scalar1=x1, scalar2=x2, op0=op0, op1=op1),
                          reads=reads, writes=[out])

    def stt(self, out, in0, scalar, in1, op0, op1):
        o, a, b = out.ap, in0.ap, in1.ap
        reads = [in0, in1]
        x = scalar
        if isinstance(scalar, V):
            reads.append(scalar)
            x = scalar.ap
        return self.S.add("dve", lambda e: e.scalar_tensor_tensor(out=o, in0=a, scalar=x, in1=b, op0=op0, op1=op1),
                          reads=reads, writes=[out])

    def cp(self, out, in_, eng="dve"):
        o, i = out.ap, in_.ap
        if eng == "act":
            return self.S.add("act", lambda e: e.copy(out=o, in_=i), reads=[in_], writes=[out])
        return self.S.add(eng, lambda e: e.tensor_copy(out=o, in_=i), reads=[in_], writes=[out])

    def memset(self, out, val, eng="dve"):
        o = out.ap
        return self.S.add(eng, lambda e: e.memset(o, val), writes=[out])

    def recip(self, out, in_):
        o, i = out.ap, in_.ap
        return self.S.add("dve", lambda e: e.reciprocal(out=o, in_=i), reads=[in_], writes=[out])

    def evac(self, out, in_, k):
        return self.cp(out, in_, eng=("act" if k % 2 else "dve"))

    def build(self):
        nc = self.nc
        P = self
        self.setup_mem()
        I = {}
        I["x"] = self.dram_in("x", [NB, SEQ, D])
        I["c"] = self.dram_in("c", [NB, D])
        I["ctx"] = self.dram_in("ctx", [NB, CTX, D])
        I["c_ctx"] = self.dram_in("c_ctx", [1, D])
        I["mod_w"] = self.dram_in("mod_w", [DEPTH, D, 3 * D])
        I["mod_b"] = self.dram_in("mod_b", [DEPTH, 3 * D])
        I["ln_g"] = self.dram_in("ln_g", [DEPTH, D])
        I["ln_b"] = self.dram_in("ln_b", [DEPTH, D])
        I["ev_w_in"] = self.dram_in("ev_w_in", [2, D, EVEN_IN])
        I["ev_ssd_conv_w"] = self.dram_in("ev_ssd_conv_w", [2, 5, 1536])
        I["ev_ssd_conv_b"] = self.dram_in("ev_ssd_conv_b", [2, 1536])
        I["ev_dt_bias"] = self.dram_in("ev_dt_bias", [2, 32])
        I["ev_a_log"] = self.dram_in("ev_a_log", [2, 32])
        I["ev_d_skip"] = self.dram_in("ev_d_skip", [2, 16])
        I["ev_ssd_norm"] = self.dram_in("ev_ssd_norm", [2, 1024])
        I["ev_cv_w"] = self.dram_in("ev_cv_w", [2, 31, 1024])
        I["ev_cv_b"] = self.dram_in("ev_cv_b", [2, 1024])
        I["ev_cv_ln_g"] = self.dram_in("ev_cv_ln_g", [2, 1024])
        I["ev_cv_ln_b"] = self.dram_in("ev_cv_ln_b", [2, 1024])
        I["ev_w_out"] = self.dram_in("ev_w_out", [2, 2048, D])
        I["od_w_in"] = self.dram_in("od_w_in", [2, D, ODD_IN])
        I["od_mlp_ln_g"] = self.dram_in("od_mlp_ln_g", [2, 1024])
        I["od_mlp_ln_b"] = self.dram_in("od_mlp_ln_b", [2, 1024])
        I["od_ws"] = self.dram_in("od_ws", [2, 8, 128, 128])
        I["od_bs"] = self.dram_in("od_bs", [2, 1024])
        I["od_sink"] = self.dram_in("od_sink", [2, 16])
        I["od_w_out"] = self.dram_in("od_w_out", [2, 2048, D])
        I["k_ident"] = self.dram_in("k_ident", [128, 128])
        I["k_tri"] = self.dram_in("k_tri", [128, 128])
        I["k_trirev"] = self.dram_in("k_trirev", [128, 128])
        I["k_negf"] = self.dram_in("k_negf", [128, 512])
        I["k_negb"] = self.dram_in("k_negb", [128, 512])
        I["k_perm"] = self.dram_in("k_perm", [128, 128])
        I["k_hmask"] = self.dram_in("k_hmask", [128, 2])
        I["k_cos"] = self.dram_in("k_cos", [128, SEQ])
        I["k_sin"] = self.dram_in("k_sin", [128, SEQ])
        self.I = I
        if self.debug:
            self.out = nc.dram_tensor("out", [NB, D, NT], F32, kind="ExternalOutput").ap()
        else:
            self.out = nc.dram_tensor("out", [NB, SEQ, D], F32, kind="ExternalOutput").ap()
        self.Idb = Buf("inputs")

        def IN(ap):
            return V(ap, [])

        self.IN = IN
        self.XT = self.scratch("XT", [NB, D, NT], F32)
        self.XBC = self.scratch("XBC", [NB, 1536, NT], BF16)
        self.U = self.scratch("U", [NB, 1024, NT], BF16)
        self.G = self.scratch("G", [NB, 1024, NT], F32)
        self.Z = self.scratch("Z", [NB, NT, 1024], F32)
        self.DT = self.scratch("DT", [NB, NT, 32], F32)
        self.MIX = self.scratch("MIX", [NB, 2048, NT], BF16)
        self.YD = self.scratch("YD", [NB, NCH, 128, 1024], F32)
        self.SF = self.scratch("SF", [NB, NCH, 128, 1024], F32)
        self.SB = self.scratch("SB", [NB, NCH, 128, 1024], F32)
        self.HF = self.scratch("HF", [NB, NCH, 128, 1024], BF16)
        self.HB = self.scratch("HB", [NB, NCH, 128, 1024], BF16)
        self.CTs = self.scratch("CTs", [NB, NCH, 128, 256], BF16)
        self.EE = self.scratch("EE", [NB, NCH, 128, 32], F32)
        self.CD = self.scratch("CD", [NB, NCH, 128, 32], F32)
        self.QT = self.scratch("QT", [NB, 1024, NT], BF16)
        self.KT = self.scratch("KT", [NB, 512, NT], BF16)
        self.VA = self.scratch("VA", [NB, NT, 256], BF16)
        self.GD = self.scratch("GD", [NB, NT, 1024], F32)
        self.AT = self.scratch("AT", [NB, 1024, NT], F32)

        self.ident = P.Pm([128, 128], F32, "ident")
        self.ident16 = P.Pm([128, 128], BF16, "ident16")
        self.tri = P.Pm([128, 128], F32, "tri")
        self.trirev = P.Pm([128, 128], F32, "trirev")
        self.negf = P.Pm([128, 512], F32, "negf")
        self.negb = P.Pm([128, 512], F32, "negb")
        self.negf16 = P.Pm([128, 512], BF16, "negf16")
        self.negb16 = P.Pm([128, 512], BF16, "negb16")
        self.perm16 = P.Pm([128, 128], BF16, "perm16")
        self.ones16 = P.Pm([128, 128], BF16, "ones16")
        self.tri16 = P.Pm([128, 128], BF16, "tri16")
        self.trirev16 = P.Pm([128, 128], BF16, "trirev16")
        self.negfb16 = P.Pm([128, 512], BF16, "negfb16")
        self.ones32 = P.Pm([128, 128], F32, "ones32")
        self.one1 = P.Pm([1, 8], F32, "one1")
        self.modT = P.Pm([128, 24, 3], F32, "modT")
        self.sc1 = P.Pm([128, 8, 3], F32, "sc1")
        self.lng = P.Pm([128, 8], F32, "lng")
        self.lnb = P.Pm([128, 8], F32, "lnb")
        self.colA = P.Pm([128, 64], F32, "colA")
        P.dma(self.ident, IN(I["k_ident"]))
        P.dma(self.ident16, IN(I["k_ident"]), eng="pool")
        P.dma(self.tri, IN(I["k_tri"]))
        P.dma(self.trirev, IN(I["k_trirev"]))
        P.dma(self.negf, IN(I["k_negf"]))
        P.dma(self.negb, IN(I["k_negb"]))
        P.dma(self.negf16, IN(I["k_negf"]), eng="pool")
        P.dma(self.negb16, IN(I["k_negb"]), eng="pool")
        P.dma(self.perm16, IN(I["k_perm"]), eng="pool")
        P.dma(self.tri16, IN(I["k_tri"]), eng="pool")
        P.dma(self.trirev16, IN(I["k_trirev"]), eng="pool")
        P.dma(self.negfb16[:, 0:128], IN(I["k_negf"][:, 0:128]), eng="pool")
        P.dma(self.negfb16[:, 128:256], IN(I["k_negb"][:, 0:128]), eng="pool")
        P.dma(self.negfb16[:, 256:384], IN(I["k_negf"][:, 0:128]), eng="pool")
        P.dma(self.negfb16[:, 384:512], IN(I["k_negb"][:, 0:128]), eng="pool")
        P.memset(self.ones32, 1.0)
        P.memset(self.ones16, 1.0)
        P.memset(self.one1, 1.0)
        self.hmask = P.Pm([128, 2], F32, "hmask")
        P.dma(self.hmask, IN(I["k_hmask"]))

        try:
            self.stage_init()
            for layer in range(self.n_layers):
                last = layer == DEPTH - 1
                self.stage_mod(layer)
                if layer % 2 == 0:
                    self.even_layer(layer // 2, layer)
                else:
                    self.odd_layer(layer // 2, layer, last)
                self.stage_out(layer, last)
        except StopIteration:
            pass
        self.stage_final()
        self.S.emit()
        self.es.close()
        return nc

    def cols(self, dst, src_row_ap, n):
        P = self
        if not getattr(self, "_rows", None):
            self._rows = [P.A([1, 1536], F32, "row%d" % i) for i in range(2)]
            self._rown = 0
        row = self._rows[self._rown % 2][:, 0:n]
        self._rown += 1
        P.dma(row, P.IN(src_row_ap))
        nchunk = n // 128
        ps = P.PS(7, 0, nchunk)
        for j in range(nchunk):
            P.mm(ps[:, j:j + 1], row[0:1, j * 128:(j + 1) * 128], self.one1[0:1, 0:1])
        P.cp(dst, ps)

    def rows_to_cols(self, dst, srcs, n):
        P = self
        R = sum(a.shape[0] for a in srcs)
        nch = n // 128
        rows = P.A([R, n], F32, "rows")
        r0 = 0
        for a in srcs:
            P.dma(rows[r0:r0 + a.shape[0], :], P.IN(a))
            r0 += a.shape[0]
        per = 512 // R
        c = 0
        while c < nch:
            m = min(per, nch - c)
            ps = P.PS(7, 0, m * R)
            for j in range(m):
                P.tr(ps[:, j * R:(j + 1) * R], rows[0:R, (c + j) * 128:(c + j + 1) * 128], self.ident[0:R, 0:R])
            P.cp(dst[:, c:c + m, :].re("p a b -> p (a b)"), ps)
            c += m

    def bcast_rows(self, dst, src_row_ap):
        n = src_row_ap.shape[-1]
        self.dma(dst, self.IN(src_row_ap.broadcast_to([128, n])))

    def load_w16(self, dst, src, ncols):
        K = dst.ap.shape[1]
        for k in range(K):
            c = 0
            while c < ncols:
                w = min(2048, ncols - c)
                self.dma(dst[:, k, c:c + w], self.IN(src[k * 128:(k + 1) * 128, c:c + w]), eng="pool")
                c += w

    def stage_init(self):
        P = self
        I = self.I
        tin = [P.A([128, 1024], F32, "tin%d" % i) for i in range(2)]
        tout = [P.A([128, 8, 128], F32, "tout%d" % i) for i in range(2)]
        n = 0
        for b in range(NB):
            for t in range(NCH):
                ti, to = tin[n % 2], tout[n % 2]
                if t < 2:
                    src = I["ctx"][b, t * 128:(t + 1) * 128, :]
                else:
                    src = I["x"][b, (t - 2) * 128:(t - 1) * 128, :]
                P.dma(ti, P.IN(src))
                pb = (n % 2) * 2
                for c in range(8):
                    P.tr(P.PS(pb, c * 128, 128), ti[:, c * 128:(c + 1) * 128], self.ident)
                P.evac(to.re("p a b -> p (a b)"), P.PS(pb, 0, 1024), n)
                P.dma(P.fm(self.XT, b, t * 128, (t + 1) * 128), to)
                n += 1
        self.stage_end()

    def stage_mod(self, layer):
        P = self
        I = self.I
        if layer % 2 == 0:
            w_, blk_ = P.A_top([128, 8, EVEN_IN], BF16, "w_in")
            P.load_w16(w_, I["ev_w_in"][layer // 2], EVEN_IN)
            self.pref["w_in"] = (w_, blk_)
        else:
            self.pref["wa"] = self.load_wa(layer // 2)
        mwb = [P.A([128, 8, 512], F32, "modw%d" % i) for i in range(2)]

        def load_mw(cb_):
            for k in range(8):
                P.dma(mwb[cb_ % 2][:, k, :], P.IN(I["mod_w"][layer, k * 128:(k + 1) * 128, cb_ * 512:(cb_ + 1) * 512]))
        load_mw(0)
        mb = P.A([1, 3072], F32, "modb")
        P.dma(mb, P.IN(I["mod_b"][layer:layer + 1, :]))
        rows = []
        for i in range(3):
            r = P.A([1, 1024], F32, "cond%d" % i)
            src = I["c"][i:i + 1, :] if i < 2 else I["c_ctx"]
            P.dma(r, P.IN(src))
            P.act(r, r, AF.Silu)
            rows.append(r)
        ps = P.PS(0, 0, 24)
        for j in range(8):
            for i in range(3):
                P.mm(ps[:, j * 3 + i:j * 3 + i + 1], rows[i][0:1, j * 128:(j + 1) * 128], self.one1[0:1, 0:1])
        condT = P.A([128, 8, 3], F32, "condT")
        P.cp(condT.re("p a b -> p (a b)"), ps)
        ps2 = P.PS(1, 0, 72)
        for oc in range(24):
            o = ps2[:, oc * 3:(oc + 1) * 3]
            if oc % 4 == 0 and oc // 4 + 1 < 6:
                load_mw(oc // 4 + 1)
            mw_ = mwb[(oc // 4) % 2]
            for k in range(8):
                P.mm(o, mw_[:, k, (oc % 4) * 128:(oc % 4 + 1) * 128], condT[:, k, :], start=(k == 0), stop=False)
            P.mm(o, mb[0:1, oc * 128:(oc + 1) * 128], self.one1[0:1, 0:3], start=False, stop=True)
        P.cp(self.modT.re("p a b -> p (a b)"), ps2)
        P.ts(self.sc1.re("p a b -> p (a b)"), self.modT[:, 8:16, :].re("p a b -> p (a b)"), 1.0, ALU.add)
        P.cols(self.lng, I["ln_g"][layer:layer + 1, :], 1024)
        P.cols(self.lnb, I["ln_b"][layer:layer + 1, :], 1024)
        self.stage_end()

    def cond_idx(self, b, t0):
        return 2 if t0 < CTX else b

    def load_hT(self, b, t0, tw, xt, hT):
        P = self
        i = self.cond_idx(b, t0)
        P.dma(xt[:, :, 0:tw], P.fm(self.XT, b, t0, t0 + tw))
        for c in range(8):
            P.act(hT[:, c, 0:tw], xt[:, c, 0:tw], AF.Identity, bias=self.modT[:, c, i:i + 1], scale=self.sc1[:, c, i:i + 1])

    def even_layer(self, li, layer):
        P = self
        I = self.I
        w, wblk = self.pref.pop("w_in")
        dtb = P.A([128, 32], F32, "dtb")
        P.bcast_rows(dtb, I["ev_dt_bias"][li:li + 1, :])
        xts = [P.A([128, 8, 512], F32, "xt%d" % i) for i in range(2)]
        hTs = [P.A([128, 8, 512], BF16, "hT%d" % i) for i in range(2)]
        sto = [P.A([128, 512], F32, "sto%d" % i) for i in range(4)]
        sto16 = [P.A([128, 512], BF16, "sto16_%d" % i) for i in range(4)]
        sig = [P.A([128, 512], F32, "sig%d" % i) for i in range(2)]
        st_z = [P.A([128, 1024], F32, "sz%d" % i) for i in range(2)]
        st_dt = [P.A([128, 128], F32, "sdt%d" % i) for i in range(2)]
        OFF_Z, OFF_XBC, OFF_DT, OFF_VAL, OFF_GT, OFF_GATE = 0, 1024, 2560, 2592, 3616, 4640
        n = 0
        nb = 0
        ns = 0

        FB = (0, 1, 2, 3, 7)
        def proj_fm(off, hT, tw, bank):
            o = P.PS(bank, 0, tw)
            for k in range(8):
                P.mm(o, w[:, k, off:off + 128], hT[:, k, 0:tw], start=(k == 0), stop=(k == 7))
            return o

        nz = 0
        for b in range(NB):
            for (t0, tw) in FM_TILES:
                xt, hT = xts[n % 2], hTs[n % 2]
                P.load_hT(b, t0, tw, xt, hT)
                for fc in range(12):
                    o = proj_fm(OFF_XBC + fc * 128, hT, tw, FB[nb % len(FB)])
                    s_ = sto16[ns % 4]
                    ns += 1
                    P.evac(s_[:, 0:tw], o, nb)
                    nb += 1
                    P.dma(P.fm(self.XBC, b, t0, t0 + tw, fc * 128, fc * 128 + 128)[:, 0, :], s_[:, 0:tw])
                for j in range(8):
                    ov = proj_fm(OFF_VAL + j * 128, hT, tw, FB[nb % len(FB)])
                    nb += 1
                    og = proj_fm(OFF_GT + j * 128, hT, tw, FB[nb % len(FB)])
                    nb += 1
                    sg_ = sig[j % 2]
                    s_ = sto16[ns % 4]
                    ns += 1
                    P.act(sg_[:, 0:tw], og, AF.Sigmoid)
                    P.tt(s_[:, 0:tw], ov, sg_[:, 0:tw], ALU.mult)
                    P.dma(P.fm(self.U, b, t0, t0 + tw, j * 128, j * 128 + 128)[:, 0, :], s_[:, 0:tw])
                for j in range(8):
                    o = proj_fm(OFF_GATE + j * 128, hT, tw, FB[nb % len(FB)])
                    nb += 1
                    s_ = sto[ns % 4]
                    ns += 1
                    P.act(s_[:, 0:tw], o, AF.Silu)
                    P.dma(P.fm(self.G, b, t0, t0 + tw, j * 128, j * 128 + 128)[:, 0, :], s_[:, 0:tw])
                nsub = tw // 128
                for sub in range(nsub):
                    tok0 = t0 + sub * 128
                    sz = st_z[nz % 2]
                    nz += 1
                    for half in range(2):
                        o = P.PS(4 + half, 0, 512)
                        for k in range(8):
                            P.mm(o, hT[:, k, sub * 128:(sub + 1) * 128], w[:, k, OFF_Z + half * 512:OFF_Z + (half + 1) * 512],
                                 start=(k == 0), stop=(k == 7))
                        P.act(sz[:, half * 512:(half + 1) * 512], o, AF.Silu)
                    P.dma(P.tm(self.Z, b, tok0, tok0 + 128), sz)
                sd = st_dt[n % 2]
                for sub in range(nsub):
                    o = P.PS(6, sub * 32, 32)
                    for k in range(8):
                        P.mm(o, hT[:, k, sub * 128:(sub + 1) * 128], w[:, k, OFF_DT:OFF_DT + 32], start=(k == 0), stop=(k == 7))
                sdv = sd[:, 0:nsub * 32]
                P.tt(sdv.re("p (s f) -> p s f", f=32), P.PS(6, 0, nsub * 32).re("p (s f) -> p s f", f=32),
                     dtb.re("p (o f) -> p o f", o=1).bc([128, nsub, 32]), ALU.add)
                P.act(sdv, sdv, AF.Exp)
                P.act(sdv, sdv, AF.Ln, bias=1.0)
                for sub in range(nsub):
                    tok0 = t0 + sub * 128
                    P.dma(P.tm(self.DT, b, tok0, tok0 + 128), sd[:, sub * 32:(sub + 1) * 32])
                n += 1
        P.release_top(wblk)
        self.stage_end()
        self.ssd_pass_a(li)
        self.conformer(li)
        self.ssd_pass_b2(li)

    def seg_bounds(self, tok):
        return (0, CTX) if tok < CTX else (CTX, NT)

    def load_window(self, dst, sc, b, t0, tw, halo):
        P = self
        lo, hi = self.seg_bounds(t0)
        a = max(t0 - halo, lo)
        e = min(t0 + tw + halo, hi)
        if a > t0 - halo:
            P.memset(dst[:, :, 0:halo], 0.0)
        if e < t0 + tw + halo:
            P.memset(dst[:, :, halo + tw:halo + tw + halo], 0.0)
        P.dma(dst[:, :, a - (t0 - halo):e - (t0 - halo)], P.fm(sc, b, a, e))

    def ssd_pass_a(self, li):
        P = self
        I = self.I
        import os
        cut = int(os.environ.get("K_CUT", "99"))
        c6 = P.A([128, 12, 6], F32, "w5cols")
        P.rows_to_cols(c6, [I["ev_ssd_conv_w"][li], I["ev_ssd_conv_b"][li:li + 1, :]], 1536)
        w5 = c6
        cbcol = c6[:, :, 5]
        cbrow = P.A([128, 1280], F32, "cbrow")
        P.bcast_rows(cbrow, I["ev_ssd_conv_b"][li:li + 1, 0:1280])
        arow = P.A([128, 32], F32, "arow")
        P.bcast_rows(arow, I["ev_a_log"][li:li + 1, :])
        P.act(arow, arow, AF.Exp)
        P.ts(arow, arow, -1.0, ALU.mult)
        dskip = P.A([128, 16], F32, "dskip")
        P.bcast_rows(dskip, I["ev_d_skip"][li:li + 1, :])
        if cut <= -2:
            self.stage_end()
            return
        diag = P.A([128, 60, 128], BF16, "diag5")
        for c in range(12):
            for k in range(5):
                P.ts(diag[:, c * 5 + k, :], self.ident16, w5[:, c, k:k + 1], ALU.mult)
        if cut <= -1:
            self.stage_end()
            return
        wins = [P.A([128, 12, 132], BF16, "win%d" % i) for i in range(2)]
        dts = [P.A([128, 32], F32, "dt%d" % i) for i in range(2)]
        sets = []
        for i in range(2):
            sets.append(dict(
                dtA=P.A([128, 32], F32, "dtA%d" % i), acum=P.A([128, 32], F32, "acum%d" % i), nacum=P.A([128, 32], F32, "nacum%d" % i),
                ee=P.A([128, 32], F32, "ee%d" % i), cd=P.A([128, 32], F32, "cd%d" % i), dte=P.A([128, 32], F32, "dte%d" % i),
                sdt=P.A([128, 32], F32, "sdt%d" % i), xsf=P.A([128, 1280], F32, "xsf%d" % i), xs=P.A([128, 1280], BF16, "xs%d" % i),
                BT=P.A([128, 2, 128], BF16, "BT%d" % i), CT=P.A([128, 2, 128], BF16, "CT%d" % i), CBT=P.A([128, 2, 128], F32, "CBT%d" % i),
                dsp=[P.A([128, 32], BF16, "dsp%d_%d" % (i, j)) for j in range(3)], dr1=P.A([128, 32], F32, "dr1_%d" % i),
                xw=[P.A([128, 1024], BF16, "xw%d_%d" % (i, j)) for j in range(2)], ydt=P.A([128, 1024], F32, "ydt%d" % i),
                ydo=P.A([128, 1024], F32, "ydo%d" % i), sfo=[P.A([128, 1024], F32, "sfo%d_%d" % (i, j)) for j in range(2)]))
        Ls = [P.A([128, 128], F32, "L%d" % i) for i in range(8)]
        Ms = [P.A([128, 128], BF16, "M%d" % i) for i in range(8)]
        cbrow16 = P.A([1, 1280], BF16, "cbrow16")
        P.dma(cbrow16, P.IN(I["ev_ssd_conv_b"][li:li + 1, 0:1280]), eng="pool")
        diagD = P.A([128, 16, 128], BF16, "diagD")
        for hd in range(16):
            P.ts(diagD[:, hd, :], self.ident16, dskip[:, hd:hd + 1], ALU.mult)
        n = 0
        nev = 0
        for b in range(NB):
            for t in range(NCH):
                tok0 = t * 128
                win, dt = wins[n % 2], dts[n % 2]
                S_ = sets[n % 2]
                dtA, acum, nacum, ee, cd, dte, sdt = S_["dtA"], S_["acum"], S_["nacum"], S_["ee"], S_["cd"], S_["dte"], S_["sdt"]
                xs, BT, CT, CBT, dsp, dr1 = S_["xs"], S_["BT"], S_["CT"], S_["CBT"], S_["dsp"], S_["dr1"]
                xw, ydo, sfo = S_["xw"], S_["ydo"], S_["sfo"]
                n += 1
                P.load_window(win, self.XBC, b, tok0, 128, 2)
                P.dma(dt, P.tm(self.DT, b, tok0, tok0 + 128))
                P.tt(dtA, dt, arow, ALU.mult)
                pa = P.PS(2, 256, 64)
                P.mm(pa[:, 0:16], self.tri, dtA[:, 0:16])
                P.mm(pa[:, 16:32], self.trirev, dtA[:, 16:32])
                P.mm(pa[:, 32:64], self.ones32, dtA)
                P.cp(acum, pa[:, 0:32])
                P.ts(nacum, pa[:, 0:32], -1.0, ALU.mult)
                P.act(ee, pa[:, 0:32], AF.Exp)
                P.act(cd, pa[:, 32:64], AF.Exp)
                P.tt(dte, pa[:, 32:64], acum, ALU.subtract)
                P.act(dte, dte, AF.Exp)
                P.tt(sdt, dte, dt, ALU.mult)
                P.dma(P.blk(self.EE, b, t), ee)
                P.dma(P.blk(self.CD, b, t), cd)
                P.cp(dsp[0], dtA)
                P.tt(dr1, dtA, dsp[0], ALU.subtract)
                P.cp(dsp[1], dr1)
                for p_, (c0, c1) in enumerate(((0, 4), (4, 8), (8, 10))):
                    for c in range(c0, c1):
                        o = P.PS(p_ % 2, (c - c0) * 128, 128)
                        P.mm(o, self.ones16[0:1, 0:128], cbrow16[0:1, c * 128:(c + 1) * 128], start=True, stop=False)
                        for k in range(5):
                            P.mm(o, win[:, c, k:k + 128], diag[:, c * 5 + k, :], start=False, stop=(k == 4))
                    P.act(xs[:, c0 * 128:c1 * 128], P.PS(p_ % 2, 0, (c1 - c0) * 128), AF.Silu)
                for c in range(8, 12):
                    o = P.PS(1, (c - 8) * 128, 128)
                    for k in range(5):
                        P.mm(o, diag[:, c * 5 + k, :], win[:, c, k:k + 128], start=(k == 0), stop=(k == 4))
                    dst = BT[:, c - 8, :] if c < 10 else CT[:, c - 10, :]
                    P.act(dst, o, AF.Silu, bias=cbcol[:, c:c + 1])
                P.dma(P.blk(self.CTs, b, t), CT.re("p a b -> p (a b)"))
                for d_ in range(2):
                    P.tt(xw[d_].re("p (h d) -> p h d", h=16), xs[:, 0:1024].re("p (h d) -> p h d", h=16),
                         sdt[:, d_ * 16:(d_ + 1) * 16].re("p (h o) -> p h o", o=1).bc([128, 16, 64]), ALU.mult,
                         eng=("dve" if d_ == 0 else "pool"))
                for g in range(2):
                    P.mm(P.PS(2, g * 128, 128), BT[:, g, :], CT[:, g, :])
                P.cp(CBT.re("p a b -> p (a b)"), P.PS(2, 0, 256))

                def seg_grp(gi):
                    sb_ = P.PS(5 + gi % 3, 0, 512)
                    P.mm(sb_, self.ident16, self.negfb16, start=True, stop=False)
                    for i_ in range(4):
                        m_ = gi * 4 + i_
                        hd_, d_ = m_ // 2, m_ % 2
                        col = d_ * 16 + hd_
                        trm = self.tri16 if d_ == 0 else self.trirev16
                        for si, sp3 in enumerate(dsp[0:2]):
                            P.mm(sb_[:, i_ * 128:(i_ + 1) * 128], sp3[:, col:col + 1].bc([128, 128]), trm,
                                 start=False, stop=(i_ == 3 and si == 1))

                def ew_grp(gi):
                    sb_ = P.PS(5 + gi % 3, 0, 512)
                    for i_ in range(4):
                        m_ = gi * 4 + i_
                        hd_, d_ = m_ // 2, m_ % 2
                        col = d_ * 16 + hd_
                        L, M = Ls[m_ % 8], Ms[m_ % 8]
                        P.act(L, sb_[:, i_ * 128:(i_ + 1) * 128], AF.Exp, bias=nacum[:, col:col + 1])
                        P.stt(M, L, dt[:, col:col + 1], CBT[:, hd_ // 8, :], ALU.mult, ALU.mult)

                def y_grp(gi):
                    for i_ in range(4):
                        m_ = gi * 4 + i_
                        hd_, d_ = m_ // 2, m_ % 2
                        yo = P.PS(3 + hd_ // 8, (hd_ % 8) * 64, 64)
                        xh = xs[:, hd_ * 64:(hd_ + 1) * 64]
                        P.mm(yo, Ms[m_ % 8], xh, start=(d_ == 0), stop=False)
                        if d_ == 1:
                            P.mm(yo, diagD[:, hd_, :], xh, start=False, stop=True)

                seg_grp(0)
                for gi in range(8):
                    if gi + 1 < 8:
                        seg_grp(gi + 1)
                    ew_grp(gi)
                    y_grp(gi)
                P.cp(ydo, P.PS(3, 0, 1024), eng="act")
                P.dma(P.blk(self.YD, b, t), ydo)
                for d_ in range(2):
                    for g in range(2):
                        bk = (2 * d_ + g) % 2
                        P.mm(P.PS(bk, 0, 512), xs[:, 1024 + g * 128:1024 + (g + 1) * 128], xw[d_][:, g * 512:(g + 1) * 512])
                        P.evac(sfo[d_][:, g * 512:(g + 1) * 512], P.PS(bk, 0, 512), nev)
                        nev += 1
                    P.dma(P.blk(self.SF if d_ == 0 else self.SB, b, t), sfo[d_])
        self.stage_end()

    def ssd_pass_b1_steps(self):
        P = self
        h = [P.A([128, 1024], F32, "h%d" % i) for i in range(2)]
        h16 = [P.A([128, 1024], BF16, "h16_%d" % i) for i in range(2)]
        sld = [P.A([128, 1024], F32, "sld%d" % i) for i in range(4)]
        cdl = [P.A([128, 32], F32, "cdl%d" % i) for i in range(4)]
        seq = []
        for b in range(NB):
            orders = [list(range(NCH)), [1, 0] + list(range(NCH - 1, 1, -1))]
            for step in range(NCH):
                for d_ in range(2):
                    seq.append((b, d_, orders[d_][step], step == 0))
        LA = 3

        def issue_loads(i):
            b, d_, t, first = seq[i]
            P.dma(sld[i % 4], P.blk(self.SF if d_ == 0 else self.SB, b, t))
            P.dma(cdl[i % 4], P.blk(self.CD, b, t))

        def mk(i):
            def f():
                b, d_, t, first = seq[i]
                if i == 0:
                    for j in range(min(LA, len(seq))):
                        issue_loads(j)
                if i + LA < len(seq):
                    issue_loads(i + LA)
                hh = h[d_]
                if first:
                    P.memset(hh, 0.0)
                hb, sl, cl = h16[i % 2], sld[i % 4], cdl[i % 4]
                P.cp(hb, hh, eng="act")
                P.dma(P.blk(self.HF if d_ == 0 else self.HB, b, t), hb)
                P.tt(hh.re("p (h d) -> p h d", h=16), hh.re("p (h d) -> p h d", h=16),
                     cl[:, d_ * 16:(d_ + 1) * 16].re("p (h o) -> p h o", o=1).bc([128, 16, 64]), ALU.mult)
                P.tt(hh, hh, sl, ALU.add)
            return f

        steps = [mk(i) for i in range(len(seq))]
        return steps

    def prefetch_w_out(self, src):
        w, blk = self.A_top([128, 16, 1024], BF16, "w_out")
        self.load_w16(w, src, 1024)
        self.pref["w_out"] = (w, blk)

    def ssd_pass_b2(self, li):
        P = self
        I = self.I
        P.prefetch_w_out(I["ev_w_out"][li])
        nrow = P.A([128, 1024], F32, "nrow")
        P.bcast_rows(nrow, I["ev_ssd_norm"][li:li + 1, :])
        cts = [P.A([128, 2, 128], BF16, "ct%d" % i) for i in range(3)]
        hfs = [P.A([128, 1024], BF16, "hf%d" % i) for i in range(3)]
        hbs = [P.A([128, 1024], BF16, "hb%d" % i) for i in range(3)]
        ees = [P.A([128, 32], F32, "ee%d" % i) for i in range(3)]
        yds = [P.A([128, 1024], F32, "yd%d" % i) for i in range(3)]
        zs = [P.A([128, 1024], F32, "z%d" % i) for i in range(3)]
        t1s = [P.A([128, 1024], F32, "t1_%d" % i) for i in range(2)]
        t2s = [P.A([128, 1024], F32, "t2_%d" % i) for i in range(2)]
        sq = P.A([128, 1024], F32, "sq")
        ybs = [P.A([128, 1024], BF16, "yb%d" % i) for i in range(2)]
        sss = [P.A([128, 1], F32, "ss%d" % i) for i in range(2)]
        rstds = [P.A([128, 1], F32, "rstd%d" % i) for i in range(2)]
        yT = [P.A([128, 8, 128], BF16, "yT%d" % i) for i in range(2)]
        n = 0
        for b in range(NB):
            for t in range(NCH):
                ct, hf, hb, ee, yd, z = cts[n % 3], hfs[n % 3], hbs[n % 3], ees[n % 3], yds[n % 3], zs[n % 3]
                t1, t2, ss, rstd = t1s[n % 2], t2s[n % 2], sss[n % 2], rstds[n % 2]
                P.dma(ct.re("p a b -> p (a b)"), P.blk(self.CTs, b, t))
                P.dma(hf, P.blk(self.HF, b, t))
                P.dma(hb, P.blk(self.HB, b, t))
                P.dma(ee, P.blk(self.EE, b, t))
                P.dma(yd, P.blk(self.YD, b, t))
                P.dma(z, P.tm(self.Z, b, t * 128, t * 128 + 128))
                pairs = (0, 2, 6)
                pf, pb_ = pairs[(2 * n) % 3], pairs[(2 * n + 1) % 3]
                for g in range(2):
                    P.mm(P.PS(pf + g, 0, 512), ct[:, g, :], hf[:, g * 512:(g + 1) * 512])
                for g in range(2):
                    P.mm(P.PS(pb_ + g, 0, 512), ct[:, g, :], hb[:, g * 512:(g + 1) * 512])
                P.tt(t1.re("p (h d) -> p h d", h=16), P.PS(pf, 0, 1024).re("p (h d) -> p h d", h=16),
                     ee[:, 0:16].re("p (h o) -> p h o", o=1).bc([128, 16, 64]), ALU.mult)
                P.tt(t2.re("p (h d) -> p h d", h=16), P.PS(pb_, 0, 1024).re("p (h d) -> p h d", h=16),
                     ee[:, 16:32].re("p (h o) -> p h o", o=1).bc([128, 16, 64]), ALU.mult)
                P.tt(t1, t1, yd, ALU.add)
                P.tt(t1, t1, t2, ALU.add)
                P.tt(t1, t1, z, ALU.mult)
                P.act(sq, t1, AF.Square, accum=ss)
                P.act(rstd, ss, AF.Sqrt, bias=EPS, scale=1.0 / 1024)
                P.recip(rstd, rstd)
                yb = ybs[n % 2]
                P.stt(yb, t1, rstd[:, 0:1], nrow, ALU.mult, ALU.mult)
                yt = yT[n % 2]
                tb_ = P.PS16(4 + n % 2)
                for c in range(8):
                    P.tr(tb_[:, c * 128:(c + 1) * 128], yb[:, c * 128:(c + 1) * 128], self.ident16)
                P.evac(yt.re("p a b -> p (a b)"), tb_, n)
                P.dma(P.fm(self.MIX, b, t * 128, t * 128 + 128, 0, 1024), yt)
                n += 1
        self.stage_end()

    def ln_fm(self, src, tw, nch, bank):
        P = self
        sq = self._ln_sq
        sq16 = self._ln_sq16
        mean, rstd, m2 = self._ln_mean, self._ln_rstd, self._ln_m2
        ps_s = P.PS(bank, 0, tw)
        ps_q = P.PS(bank + 1, 0, tw)
        for c in range(nch):
            P.cp(sq16[c % 2][:, 0:tw], src[:, c, 0:tw], eng="pool")
            P.mm(ps_s, self.ones16, sq16[c % 2][:, 0:tw], start=(c == 0), stop=(c == nch - 1))
        for c in range(nch):
            P.act(sq[c % 2][:, 0:tw], src[:, c, 0:tw], AF.Square)
            P.mm(ps_q, self.ones16, sq[c % 2][:, 0:tw], start=(c == 0), stop=(c == nch - 1))
        inv = 1.0 / (nch * 128)
        P.act(mean[:, 0:tw], ps_s, AF.Copy, scale=inv)
        P.tt(m2[:, 0:tw], mean[:, 0:tw], mean[:, 0:tw], ALU.mult)
        P.stt(rstd[:, 0:tw], ps_q, inv, m2[:, 0:tw], ALU.mult, ALU.subtract)
        P.act(rstd[:, 0:tw], rstd[:, 0:tw], AF.Sqrt, bias=EPS)
        P.recip(rstd[:, 0:tw], rstd[:, 0:tw])
        return mean, rstd

    def ln_alloc(self):
        P = self
        self._ln_sq = [P.A([128, 512], BF16, "lnsq%d" % i) for i in range(2)]
        self._ln_sq16 = [P.A([128, 512], BF16, "lnsq16_%d" % i) for i in range(2)]
        self._ln_mean = P.A([128, 512], F32, "lnmean")
        self._ln_rstd = P.A([128, 512], F32, "lnrstd")
        self._ln_m2 = P.A([128, 512], F32, "lnm2")

    def conformer(self, li):
        P = self
        I = self.I
        cols = P.A([128, 8, 34], F32, "cvcols")
        P.rows_to_cols(cols, [I["ev_cv_w"][li], I["ev_cv_b"][li:li + 1, :], I["ev_cv_ln_g"][li:li + 1, :],
                              I["ev_cv_ln_b"][li:li + 1, :]], 1024)
        wc = cols
        cvb, cg, cb = cols[:, :, 31], cols[:, :, 32], cols[:, :, 33]
        diag = P.A([128, 248, 128], BF16, "diag31")
        for c in range(8):
            for k in range(31):
                P.ts(diag[:, c * 31 + k, :], self.ident16, wc[:, c, k:k + 1], ALU.mult)
        self.ln_alloc()
        wins = [P.A([128, 8, 542], BF16, "cwin%d" % i) for i in range(2)]
        gts = [P.A([128, 8, 512], F32, "gt0")] * 2
        hcs = [P.A([128, 8, 512], F32, "hc0")] * 2
        b1 = self.ssd_pass_b1_steps()
        tmp = [P.A([128, 512], F32, "ctmp%d" % i) for i in range(2)]
        so = [P.A([128, 8, 512], BF16, "cso0")] * 2
        n = 0
        ncb = 0
        for b in range(NB):
            for (t0, tw) in FM_TILES:
                win, gt, o_, hc = wins[n % 2], gts[n % 2], so[n % 2], hcs[n % 2]
                n += 1
                wv = win[:, :, 0:tw + 30]
                P.load_window(wv, self.U, b, t0, tw, 15)
                P.dma(gt[:, :, 0:tw], P.fm(self.G, b, t0, t0 + tw))
                for c in range(8):
                    o = P.PS((0, 1, 4, 5, 6, 7)[ncb % 6], 0, tw)
                    ncb += 1
                    for k in range(31):
                        P.mm(o, diag[:, c * 31 + k, :], wv[:, c, k:k + tw], start=(k == 0), stop=(k == 30))
                    P.act(hc[:, c, 0:tw], o, AF.Identity, bias=cvb[:, c:c + 1])
                    if b1:
                        b1.pop(0)()
                mean, rstd = P.ln_fm(hc, tw, 8, 2)
                for c in range(8):
                    tp = tmp[c % 2]
                    P.tt(tp[:, 0:tw], hc[:, c, 0:tw], mean[:, 0:tw], ALU.subtract)
                    P.tt(tp[:, 0:tw], tp[:, 0:tw], rstd[:, 0:tw], ALU.mult)
                    P.act(tp[:, 0:tw], tp[:, 0:tw], AF.Silu, bias=cb[:, c:c + 1], scale=cg[:, c:c + 1])
                    P.tt(o_[:, c, 0:tw], tp[:, 0:tw], gt[:, c, 0:tw], ALU.mult)
                P.dma(P.fm(self.MIX, b, t0, t0 + tw, 1024, 2048), o_[:, :, 0:tw])
        while b1:
            b1.pop(0)()
        self.stage_end()

    def stage_out(self, layer, last):
        P = self
        I = self.I
        src = (I["ev_w_out"] if layer % 2 == 0 else I["od_w_out"])[layer // 2]
        if "w_out" in self.pref:
            w, wblk = self.pref.pop("w_out")
        else:
            w, wblk = P.A_top([128, 16, 1024], BF16, "w_out")
            P.load_w16(w, src, 1024)
        self.ln_alloc()
        mixs = [P.A([128, 16, 512], BF16, "mix%d" % i) for i in range(2)]
        xts = [P.A([128, 8, 512], F32, "oxt0")] * 2
        rs_ = [P.A([128, 8, 512], F32, "r%d" % i) for i in range(2)]
        tmp = [P.A([128, 512], F32, "otmp%d" % i) for i in range(2)]
        xo = [P.A([128, 8, 512], F32, "xo%d" % i) for i in range(2)]
        tok = [P.A([128, 1024], F32, "tok%d" % i) for i in range(2)]
        n = 0
        nt = 0
        nob = 0
        tmp2 = [P.A([128, 512], F32, "otmpb%d" % i) for i in range(2)]
        for b in range(NB):
            for (t0, tw) in FM_TILES:
                if last and t0 < CTX:
                    continue
                i = self.cond_idx(b, t0)
                mix, xt, xo_, r = mixs[n % 2], xts[n % 2], xo[n % 2], rs_[n % 2]
                n += 1
                P.dma(mix[:, :, 0:tw], P.fm(self.MIX, b, t0, t0 + tw))
                P.dma(xt[:, :, 0:tw], P.fm(self.XT, b, t0, t0 + tw))
                obanks = (0, 1) if (last and not self.debug) else (0, 1, 4, 5, 6, 7)
                for oc in range(8):
                    o = P.PS(obanks[nob % len(obanks)], 0, tw)
                    nob += 1
                    for k in range(16):
                        P.mm(o, w[:, k, oc * 128:(oc + 1) * 128], mix[:, k, 0:tw], start=(k == 0), stop=(k == 15))
                    tp = tmp[oc % 2]
                    P.act(tp[:, 0:tw], xt[:, oc, 0:tw], AF.Copy, scale=ALPHA)
                    P.stt(r[:, oc, 0:tw], o, self.modT[:, 16 + oc, i:i + 1], tp[:, 0:tw], ALU.mult, ALU.add)
                mean, rstd = P.ln_fm(r, tw, 8, 2)
                for c in range(8):
                    tp = tmp2[c % 2]
                    P.tt(tp[:, 0:tw], r[:, c, 0:tw], mean[:, 0:tw], ALU.subtract, eng="pool")
                    P.tt(tp[:, 0:tw], tp[:, 0:tw], rstd[:, 0:tw], ALU.mult)
                    P.act(xo_[:, c, 0:tw], tp[:, 0:tw], AF.Identity, bias=self.lnb[:, c:c + 1], scale=self.lng[:, c:c + 1])
                if last and not self.debug:
                    for sub in range(tw // 128):
                        tk = tok[nt % 2]
                        pb = 4 + (nt % 2) * 2
                        nt += 1
                        for c in range(8):
                            P.tr(P.PS(pb, c * 128, 128), xo_[:, c, sub * 128:(sub + 1) * 128], self.ident)
                        P.evac(tk, P.PS(pb, 0, 1024), nt)
                        tok0 = t0 - CTX + sub * 128
                        op = P.dma(P._dv(V(self.out[b, tok0:tok0 + 128, :], [])), tk)
                        self.S.out_dmas.append(op)
                else:
                    P.dma(P.fm(self.XT, b, t0, t0 + tw), xo_[:, :, 0:tw])
        P.release_top(wblk)
        self.stage_end()

    def stage_final(self):
        P = self
        if not self.debug:
            return
        buf = [P.A([128, 8, 512], F32, "dbg%d" % i) for i in range(2)]
        n = 0
        for b in range(NB):
            for (t0, tw) in FM_TILES:
                bb = buf[n % 2]
                n += 1
                P.dma(bb[:, :, 0:tw], P.fm(self.XT, b, t0, t0 + tw))
                op = P.dma(V(self.out[b, :, t0:t0 + tw].rearrange("(c p) t -> p c t", p=128), []), bb[:, :, 0:tw])
                self.S.out_dmas.append(op)

    def load_wa(self, li):
        P = self
        I = self.I
        wa, blk = P.A_top([128, 8, 3584], BF16, "w_in_a")
        for k in range(8):
            r = I["od_w_in"][li, k * 128:(k + 1) * 128, :]
            P.dma(wa[:, k, 0:1024], P.IN(r[:, 0:1024]), eng="pool")
            P.dma(wa[:, k, 1024:2048], P.IN(r[:, 2048:3072]), eng="pool")
            P.dma(wa[:, k, 2048:3072], P.IN(r[:, 3072:4096]), eng="pool")
            for kv in range(4):
                P.dma(wa[:, k, 3072 + kv * 128:3072 + (kv + 1) * 128].re("p (o d) -> p o d", o=2),
                      P.IN(r[:, 4096 + kv * 64:4096 + (kv + 1) * 64].rearrange("p (o d) -> p o d", o=1).broadcast_to([128, 2, 64])), eng="pool")
        return wa, blk

    def load_wb(self, li):
        P = self
        I = self.I
        wb, blk = P.A_top([128, 8, 2304], BF16, "w_in_b")
        for k in range(8):
            r = I["od_w_in"][li, k * 128:(k + 1) * 128, :]
            P.dma(wb[:, k, 0:1024], P.IN(r[:, 1024:2048]), eng="pool")
            P.dma(wb[:, k, 1024:1280], P.IN(r[:, 4352:4608]), eng="pool")
            P.dma(wb[:, k, 1280:2304], P.IN(r[:, 4608:5632]), eng="pool")
        return wb, blk

    def odd_layer(self, li, layer, last):
        P = self
        I = self.I
        wa, wablk = self.pref.pop("wa")
        self.pref["wb"] = self.load_wb(li)
        cos = P.A([128, 512], F32, "cos")
        sin = P.A([128, 512], F32, "sin")
        xt = P.A([128, 8, 512], F32, "xt0")
        hTs = [P.A([128, 8, 512], BF16, "hT%d" % i) for i in range(2)]
        t_a = [P.A([128, 512], F32, "ta%d" % i) for i in range(2)]
        t_b = [P.A([128, 512], F32, "tb%d" % i) for i in range(2)]
        gu = P.A([128, 8, 512], F32, "gu")
        q16 = [P.A([128, 512], BF16, "q16_%d" % i) for i in range(2)]
        sto = [P.A([128, 512], F32, "sto%d" % i) for i in range(4)]
        sto16 = [P.A([128, 512], BF16, "sto16_%d" % i) for i in range(4)]
        n = 0
        nb = 0
        ns = 0

        FB = (0, 1, 2, 3, 6, 7)
        def proj_fm(lhs_fn, hT, tw, bank):
            o = P.PS(bank, 0, tw)
            for k in range(8):
                P.mm(o, lhs_fn(k), hT[:, k, 0:tw], start=(k == 0), stop=(k == 7))
            return o

        def rope_store(o, dst, tw, t0, j):
            if t0 < CTX:
                P.evac(dst, o, j)
                return
            qb = q16[j % 2]
            P.cp(qb[:, 0:tw], o, eng="act")
            sw = P.PS(4 + (j % 2), 0, tw)
            P.mm(sw, self.perm16, qb[:, 0:tw])
            ta, tb = t_a[j % 2], t_b[j % 2]
            P.tt(ta[:, 0:tw], o, cos[:, 0:tw], ALU.mult)
            P.tt(tb[:, 0:tw], sw, sin[:, 0:tw], ALU.mult)
            P.tt(dst, ta[:, 0:tw], tb[:, 0:tw], ALU.add)

        for b in range(NB):
            for (t0, tw) in FM_TILES:
                hT = hTs[n % 2]
                n += 1
                P.load_hT(b, t0, tw, xt, hT)
                if t0 >= CTX:
                    P.dma(cos[:, 0:tw], P.IN(I["k_cos"][:, t0 - CTX:t0 - CTX + tw]))
                    P.dma(sin[:, 0:tw], P.IN(I["k_sin"][:, t0 - CTX:t0 - CTX + tw]))
                for j in range(8):
                    ou = proj_fm(lambda k: wa[:, k, j * 128:(j + 1) * 128], hT, tw, FB[nb % len(FB)])
                    nb += 1
                    P.act(gu[:, j, 0:tw], ou, AF.Gelu_apprx_tanh)
                for j in range(8):
                    og = proj_fm(lambda k: wa[:, k, 1024 + j * 128:1024 + (j + 1) * 128], hT, tw, FB[nb % len(FB)])
                    nb += 1
                    tb = t_b[j % 2]
                    s_ = sto[ns % 4]
                    ns += 1
                    P.act(tb[:, 0:tw], og, AF.Silu)
                    P.tt(s_[:, 0:tw], gu[:, j, 0:tw], tb[:, 0:tw], ALU.mult)
                    P.dma(P.fm(self.AT, b, t0, t0 + tw, j * 128, j * 128 + 128)[:, 0, :], s_[:, 0:tw])
                for j in range(8):
                    o = proj_fm(lambda k: wa[:, k, 2048 + j * 128:2048 + (j + 1) * 128], hT, tw, FB[nb % len(FB)])
                    nb += 1
                    s_ = sto16[ns % 4]
                    ns += 1
                    rope_store(o, s_[:, 0:tw], tw, t0, j)
                    P.dma(P.fm(self.QT, b, t0, t0 + tw, j * 128, j * 128 + 128)[:, 0, :], s_[:, 0:tw])
                for kv in range(4):
                    o = proj_fm(lambda k: wa[:, k, 3072 + kv * 128:3072 + (kv + 1) * 128], hT, tw, FB[nb % len(FB)])
                    nb += 1
                    s_ = sto16[ns % 4]
                    ns += 1
                    rope_store(o, s_[:, 0:tw], tw, t0, kv)
                    P.dma(P.fm(self.KT, b, t0, t0 + tw, kv * 128, kv * 128 + 128)[:, 0, :], s_[:, 0:tw])
        P.release_top(wablk)
        self.stage_end()

        wb, wbblk = self.pref.pop("wb")
        lgrow = P.A([128, 1024], F32, "lgrow")
        lbrow = P.A([128, 1024], F32, "lbrow")
        bsrow = P.A([128, 1024], F32, "bsrow")
        P.bcast_rows(lgrow, I["od_mlp_ln_g"][li:li + 1, :])
        P.bcast_rows(lbrow, I["od_mlp_ln_b"][li:li + 1, :])
        P.bcast_rows(bsrow, I["od_bs"][li:li + 1, :])
        wsT = P.A([128, 8, 128], BF16, "wsT")
        wstmp = [P.A([128, 128], F32, "wstmp%d" % i) for i in range(2)]
        for g in range(8):
            P.dma(wstmp[g % 2], P.IN(I["od_ws"][li, g]))
            P.tr(P.PS(g % 2, 0, 128), wstmp[g % 2], self.ident)
            P.evac(wsT[:, g, :], P.PS(g % 2, 0, 128), g)
        xt = P.A([128, 8, 512], F32, "xt0")
        hTs = [P.A([128, 8, 512], BF16, "hT%d" % i) for i in range(2)]
        ats = [P.A([128, 8, 128], F32, "at%d" % i) for i in range(2)]
        vfs = [P.A([128, 1024], F32, "vf%d" % i) for i in range(2)]
        vns = [P.A([128, 1024], BF16, "vn%d" % i) for i in range(2)]
        sts = [P.A([128, 2, 6], F32, "bnst%d" % i) for i in range(2)]
        mvs = [P.A([128, 2], F32, "bnmv%d" % i) for i in range(2)]
        rss = [P.A([128, 1], F32, "bnrs%d" % i) for i in range(2)]
        mxos = [P.A([128, 1024], F32, "mxo0")] * 2
        smx = [P.A([128, 8, 128], BF16, "smx%d" % i) for i in range(2)]
        sva = [P.A([128, 256], BF16, "sva%d" % i) for i in range(2)]
        sgd = [P.A([128, 1024], F32, "sgd%d" % i) for i in range(2)]
        n = 0
        cnt = {"bank": 0}

        def nbank():
            cnt["bank"] += 1
            return (cnt["bank"] - 1) % 6

        def part_a(b, tok0, hT, sub, nz):
            hs = lambda k: hT[:, k, sub * 128:(sub + 1) * 128]
            at = ats[nz % 2]
            vf, vn, st, mv, rs = vfs[nz % 2], vns[nz % 2], sts[nz % 2], mvs[nz % 2], rss[nz % 2]
            P.dma(at, P.fm(self.AT, b, tok0, tok0 + 128))
            for half in range(2):
                o = P.PS(nbank(), 0, 512)
                for k in range(8):
                    P.mm(o, hs(k), wb[:, k, half * 512:(half + 1) * 512], start=(k == 0), stop=(k == 7))
                P.act(vf[:, half * 512:(half + 1) * 512], o, AF.Gelu_apprx_tanh)
                h0 = half * 512
                stv, vfv = st.ap[:, half, :], vf.ap[:, h0:h0 + 512]
                self.S.add("dve", lambda e, stv=stv, vfv=vfv: e.bn_stats(out=stv, in_=vfv), reads=[vf], writes=[st])
            sta, mva = st.ap.rearrange("p a b -> p (a b)"), mv.ap
            self.S.add("dve", lambda e, sta=sta, mva=mva: e.bn_aggr(out=mva, in_=sta), reads=[st], writes=[mv])
            P.act(rs, mv[:, 1:2], AF.Sqrt, bias=EPS)
            P.recip(rs, rs)
            P.ts(vf, vf, mv[:, 0:1], ALU.subtract, rs[:, 0:1], ALU.mult)
            P.tt(vf, vf, lgrow, ALU.mult)
            P.tt(vn, vf, lbrow, ALU.add)
            o = P.PS(nbank(), 0, 256)
            for k in range(8):
                P.mm(o, hs(k), wb[:, k, 1024:1280], start=(k == 0), stop=(k == 7))
            sv = sva[nz % 2]
            P.cp(sv, o, eng="act")
            P.dma(P.tm(self.VA, b, tok0, tok0 + 128), sv)
            sg = sgd[nz % 2]
            for half in range(2):
                o = P.PS(nbank(), 0, 512)
                for k in range(8):
                    P.mm(o, hs(k), wb[:, k, 1280 + half * 512:1280 + (half + 1) * 512], start=(k == 0), stop=(k == 7))
                P.act(sg[:, half * 512:(half + 1) * 512], o, AF.Silu)
            P.dma(P.tm(self.GD, b, tok0, tok0 + 128), sg)

        def part_b(b, tok0, nz):
            at, vn, mxo = ats[nz % 2], vns[nz % 2], mxos[nz % 2]
            for g in range(8):
                P.mm(P.PS(6 + g // 4, (g % 4) * 128, 128), vn[:, g * 128:(g + 1) * 128], wsT[:, g, :])
            P.tt(mxo, P.PS(6, 0, 1024), bsrow, ALU.add)
            sm = smx[nz % 2]
            P.tt(sm, mxo.re("p (g q) -> p g q", g=8), at, ALU.mult)
            P.dma(P.fm(self.MIX, b, tok0, tok0 + 128, 0, 1024), sm)

        work = []
        for b in range(NB):
            for (t0, tw) in FM_TILES:
                for sub in range(tw // 128):
                    work.append((b, t0, tw, sub))
        cur_hT = None
        pend = None
        for nz, (b, t0, tw, sub) in enumerate(work):
            if sub == 0:
                cur_hT = hTs[n % 2]
                n += 1
                P.load_hT(b, t0, tw, xt, cur_hT)
            tok0 = t0 + sub * 128
            part_a(b, tok0, cur_hT, sub, nz)
            if pend is not None:
                part_b(*pend)
            pend = (b, tok0, nz)
        part_b(*pend)
        P.release_top(wbblk)
        self.stage_end()
        self.attention(li, last)

    def attention(self, li, last):
        P = self
        I = self.I
        P.prefetch_w_out(I["od_w_out"][li])
        esink = P.A([128, 16], F32, "esink")
        P.bcast_rows(esink, I["od_sink"][li:li + 1, :])
        P.act(esink, esink, AF.Exp)
        kc = P.A([128, 4, 256], BF16, "kctx")
        vc = P.A([128, 2, 4, 65], BF16, "vctx")
        qts = [P.A([128, 8, 128], BF16, "qt%d" % i) for i in range(2)]
        qms = [P.A([128, 16, 128], BF16, "qm%d" % i) for i in range(2)]
        kws = [P.A([128, 4, 384], BF16, "kw%d" % i) for i in range(2)]
        vws = [P.A([128, 3, 4, 65], BF16, "vw%d" % i) for i in range(2)]
        gds = [P.A([128, 1024], F32, "gd%d" % i) for i in range(2)]
        den = P.A([128, 16], F32, "den")
        yd = P.A([128, 1024], F32, "yd")
        yT = [P.A([128, 8, 128], BF16, "ayT%d" % i) for i in range(2)]
        for t_ in vws + [vc]:
            P.memset(t_, 1.0)
        E = [P.A([128, 5, 512], BF16, "E%d" % i) for i in range(3)]
        cnt = {"n": 0, "ns": 0, "ne": 0, "ev": 0}

        def prep(b, t):
            n = cnt["n"]
            cnt["n"] += 1
            qt, kw, vw, gd, qm = qts[n % 2], kws[n % 2], vws[n % 2], gds[n % 2], qms[n % 2]
            tok0 = t * 128
            P.dma(qt, P.fm(self.QT, b, tok0, tok0 + 128))
            P.dma(gd, P.tm(self.GD, b, tok0, tok0 + 128))
            for e_ in range(2):
                P.ts(qm.re("p (j e) q -> p j e q", e=2)[:, :, e_, :], qt, self.hmask[:, e_:e_ + 1], ALU.mult)
            kbs = []
            if t >= 2:
                lo = max(t - 1, 2)
                hi = min(t + 1, NCH - 1)
                P.dma(kw[:, :, 0:(hi - lo + 1) * 128], P.fm(self.KT, b, lo * 128, (hi + 1) * 128))
                for j, tt_ in enumerate(range(lo, hi + 1)):
                    P.dma(vw[:, j, :, 0:64], P.tm(self.VA, b, tt_ * 128, (tt_ + 1) * 128).re("p (h d) -> p h d", h=4))
                    mask = None
                    if tt_ == t - 1:
                        mask = self.negb16
                    elif tt_ == t + 1:
                        mask = self.negf16
                    kbs.append((kw, vw, j, mask))
            kbs.append((kc, vc, 0, None))
            kbs.append((kc, vc, 1, None))
            return dict(b=b, t=t, qm=qm, gd=gd, kbs=kbs, n=n, E={})

        def st(c, kv):
            Et = E[cnt["ne"] % 3]
            cnt["ne"] += 1
            c["E"][kv] = Et
            for jj, (ksrc, vsrc, j, mask) in enumerate(c["kbs"]):
                sp_ = P.PS(cnt["ns"] % 4, 0, 512)
                cnt["ns"] += 1
                if mask is not None:
                    P.mm(sp_, self.ident16, mask, start=True, stop=False)
                P.mm(sp_, ksrc[:, kv, j * 128:(j + 1) * 128],
                     c["qm"][:, kv * 4:(kv + 1) * 4, :].re("p h q -> p (h q)"), start=(mask is None), stop=True)
                P.act(Et[:, jj, :], sp_, AF.Exp, scale=ATT_SCALE)

        def pv(c, kv):
            Et = c["E"][kv]
            nk = len(c["kbs"])
            for r in range(4):
                h = kv * 4 + r
                o = P.PS(4 + h // 7, (h % 7) * 65, 65)
                for jj, (ksrc, vsrc, j, mask) in enumerate(c["kbs"]):
                    P.mm(o, Et[:, jj, r * 128:(r + 1) * 128], vsrc[:, j, kv, :], start=(jj == 0), stop=(jj == nk - 1))

        def fin(c):
            gd = c["gd"]
            for bank, h0, nh in ((4, 0, 7), (5, 7, 7), (6, 14, 2)):
                ov = P.PS(bank, 0, nh * 65).re("p (h d) -> p h d", d=65)
                P.tt(den[:, h0:h0 + nh].re("p (h o) -> p h o", o=1), ov[:, :, 64:65],
                     esink[:, h0:h0 + nh].re("p (h o) -> p h o", o=1), ALU.add)
            P.recip(den, den)
            for bank, h0, nh in ((4, 0, 7), (5, 7, 7), (6, 14, 2)):
                ov = P.PS(bank, 0, nh * 65).re("p (h d) -> p h d", d=65)
                P.tt(yd[:, h0 * 64:(h0 + nh) * 64].re("p (h d) -> p h d", d=64), ov[:, :, 0:64],
                     den[:, h0:h0 + nh].re("p (h o) -> p h o", o=1).bc([128, nh, 64]), ALU.mult)
            ydg = ydgs[c["n"] % 2]
            P.tt(ydg, yd, gd, ALU.mult, eng="pool")
            yt = yT[c["n"] % 2]
            tb_ = P.PS16(7)
            for cc in range(8):
                P.tr(tb_[:, cc * 128:(cc + 1) * 128], ydg[:, cc * 128:(cc + 1) * 128], self.ident16)
            P.evac(yt.re("p a b -> p (a b)"), tb_, cnt["ev"])
            cnt["ev"] += 1
            tok0 = c["t"] * 128
            P.dma(P.fm(self.MIX, c["b"], tok0, tok0 + 128, 1024, 2048), yt)

        ydgs = [P.A([128, 1024], BF16, "ydg%d" % i) for i in range(2)]
        for b in range(NB):
            P.dma(kc, P.fm(self.KT, b, 0, CTX))
            for j in range(2):
                P.dma(vc[:, j, :, 0:64], P.tm(self.VA, b, j * 128, (j + 1) * 128).re("p (h d) -> p h d", h=4))
            qblocks = list(range(2, NCH)) + ([] if last else [0, 1])
            items = []
            for t in qblocks:
                for kv in range(4):
                    items.append((t, kv))
            ctxs = {}
            ctxs[items[0][0]] = prep(b, items[0][0])
            st(ctxs[items[0][0]], 0)
            for i_, (t, kv) in enumerate(items):
                if i_ + 1 < len(items):
                    t2, kv2 = items[i_ + 1]
                    if t2 not in ctxs:
                        ctxs[t2] = prep(b, t2)
                    st(ctxs[t2], kv2)
                pv(ctxs[t], kv)
                if kv == 3:
                    fin(ctxs[t])
        self.stage_end()


def _consts():
    i = np.arange(128)
    tri = (i[:, None] <= i[None, :]).astype(np.float32)
    trirev = (i[:, None] >= i[None, :]).astype(np.float32)
    negf = np.where(i[None, :] >= i[:, None], 0.0, NEG).astype(np.float32)
    negb = np.where(i[None, :] <= i[:, None], 0.0, NEG).astype(np.float32)
    perm = np.zeros((128, 128), np.float32)
    for po in range(128):
        d = po % 64
        if d < 32:
            perm[po + 32, po] = -1.0
        else:
            perm[po - 32, po] = 1.0
    t = np.arange(SEQ)
    row = (t // 64).astype(np.float32)
    col = (t % 64).astype(np.float32)
    inv = (10000.0 ** (-np.arange(16, dtype=np.float32) / 16)).astype(np.float32)
    ang = np.concatenate([row[:, None] * inv, col[:, None] * inv], -1).astype(np.float32)
    p = np.arange(128)
    cosT = np.cos(ang).astype(np.float32)[:, (p % 64) % 32].T.copy()
    sinT = np.sin(ang).astype(np.float32)[:, (p % 64) % 32].T.copy()
    hmask = np.zeros((128, 2), np.float32)
    hmask[:64, 0] = 1.0
    hmask[64:, 1] = 1.0
    return dict(k_hmask=hmask, k_ident=np.eye(128, dtype=np.float32), k_tri=tri, k_trirev=trirev,
                k_negf=np.tile(negf, (1, 4)), k_negb=np.tile(negb, (1, 4)), k_perm=perm,
                k_cos=np.ascontiguousarray(cosT), k_sin=np.ascontiguousarray(sinT))


_CACHE = {}


def _get_nc(n_layers=DEPTH, debug=False):
    key = (n_layers, debug)
    if key not in _CACHE:
        _CACHE[key] = Prog(n_layers, debug).build()
    return _CACHE[key]


def make_in_maps(inputs, n_cores):
    f = lambda a: np.ascontiguousarray(np.asarray(a, dtype=np.float32))
    shared = {}
    for k, v in inputs.items():
        if k in ("x", "c", "ctx"):
            continue
        v = f(v)
        if k == "c_ctx":
            v = v.reshape(1, D)
        elif k == "ev_a_log":
            v = v.reshape(2, 32)
        elif k == "od_bs":
            v = v.reshape(2, 1024)
        shared[k] = v
    shared.update(_consts())
    x, c, ctx = f(inputs["x"]), f(inputs["c"]), f(inputs["ctx"])
    maps = []
    for i in range(n_cores):
        m = dict(shared)
        m["x"] = x[i * NB:(i + 1) * NB]
        m["c"] = c[i * NB:(i + 1) * NB]
        m["ctx"] = ctx[i * NB:(i + 1) * NB]
        maps.append(m)
    return maps


def kernel(**inputs):
    nc = _get_nc()
    n_cores = 8
    maps = make_in_maps(inputs, n_cores)
    res = run_bass_kernel_spmd(nc, maps, core_ids=list(range(n_cores)))
    return np.concatenate([r["out"] for r in res.results], axis=0).astype(np.float32)
```
